# Optimizing a Trainium2 kernel written in Bass

```python
import jax, jax.numpy as jnp
from jax import lax
import numpy as np

D_MODEL = 1024
BATCH = 8
SEQ = 2048
DEPTH = 2

HEAD_DIM = 64
BLOCK = 128
ROPE_THETA = 10000.0
EPS = 1e-6
FOX_HEADS = 8
DSA_HEADS = 8
IDX_HEADS = 4
IDX_DIM = 64
DSA_TOPK_MAX = 256
RET_HEADS = 8
SWA_Q_HEADS = 8
SWA_KV_HEADS = 2
SWA_WINDOW = 128
N_BRANCH = 4
BRANCH_WIDTH = 8 * HEAD_DIM
D_FF = 2816
CONV_WIDTH = 3

COL_SIZES = (
    FOX_HEADS * HEAD_DIM, FOX_HEADS * HEAD_DIM, FOX_HEADS * HEAD_DIM, FOX_HEADS,
    DSA_HEADS * HEAD_DIM, HEAD_DIM, HEAD_DIM, IDX_HEADS * IDX_DIM, IDX_DIM, IDX_HEADS,
    RET_HEADS * HEAD_DIM, RET_HEADS * HEAD_DIM, RET_HEADS * HEAD_DIM, RET_HEADS * HEAD_DIM,
    SWA_Q_HEADS * HEAD_DIM, SWA_KV_HEADS * HEAD_DIM, SWA_KV_HEADS * HEAD_DIM,
    N_BRANCH * D_MODEL,
)
N_IN = sum(COL_SIZES)

kernel_name = "hybrid_fox_dsa_retention_swa_gated"


def rmsnorm(x, g):
    x32 = x.astype(jnp.float32)
    y = x32 * lax.rsqrt(jnp.mean(x32 * x32, axis=-1, keepdims=True) + EPS)
    return (y * g.astype(jnp.float32)).astype(x.dtype)


def rope(x, pos):
    d = x.shape[-1]
    half = d // 2
    inv = 1.0 / (ROPE_THETA ** (jnp.arange(half, dtype=jnp.float32) * 2.0 / d))
    ang = pos[:, None] * inv[None, :]
    cos = jnp.cos(ang)[None, :, None, :]
    sin = jnp.sin(ang)[None, :, None, :]
    x32 = x.astype(jnp.float32)
    x1, x2 = x32[..., :half], x32[..., half:]
    return jnp.concatenate([x1 * cos - x2 * sin, x2 * cos + x1 * sin], axis=-1).astype(x.dtype)


def to_blocks(t, nb):
    return t.reshape(t.shape[0], nb, BLOCK, *t.shape[2:]).swapaxes(0, 1)


def from_blocks(t):
    t = t.swapaxes(0, 1)
    return t.reshape(t.shape[0], t.shape[1] * t.shape[2], -1)


def forgetting_attention(q, k, v, f_logit, f_bias):
    B, S, H, d = q.shape
    nb = S // BLOCK
    neg = jnp.finfo(jnp.float32).min
    logf = jax.nn.log_sigmoid((f_logit + f_bias).astype(jnp.float32))
    F = jnp.cumsum(logf, axis=1).transpose(0, 2, 1)
    Fq_blocks = F.reshape(B, H, nb, BLOCK).transpose(2, 0, 1, 3)
    key_pos = jnp.arange(S, dtype=jnp.int32)
    scale = d ** -0.5

    def blk(inp):
        qb, Fq, off = inp
        t = off + jnp.arange(BLOCK, dtype=jnp.int32)
        logits = (jnp.einsum('bqhd,bshd->bhqs', qb, k).astype(jnp.float32) * scale
                  + Fq[..., None] - F[:, :, None, :])
        causal = key_pos[None, :] <= t[:, None]
        p = jax.nn.softmax(jnp.where(causal[None, None], logits, neg), axis=-1)
        return jnp.einsum('bhqs,bshd->bqhd', p.astype(v.dtype), v)

    offs = jnp.arange(nb, dtype=jnp.int32) * BLOCK
    out = lax.map(blk, (to_blocks(q, nb), Fq_blocks, offs))
    return from_blocks(out)


def dsa_attention(q, k, v, qi, ki, wi):
    B, S, H, d = q.shape
    nb = S // BLOCK
    topk = min(DSA_TOPK_MAX, S // 4)
    neg = jnp.finfo(jnp.float32).min
    key_pos = jnp.arange(S, dtype=jnp.int32)
    bidx = jnp.arange(B, dtype=jnp.int32)[:, None, None]
    ki32 = ki.astype(jnp.float32)
    w_scale = (IDX_HEADS ** -0.5) * (IDX_DIM ** -0.5)
    scale = d ** -0.5

    def blk(inp):
        qb, qib, wb, off = inp
        t = off + jnp.arange(BLOCK, dtype=jnp.int32)
        dots = jnp.einsum('bqhd,bsd->bqhs', qib.astype(jnp.float32), ki32)
        score = jnp.einsum('bqhs,bqh->bqs', jax.nn.relu(dots), wb.astype(jnp.float32) * w_scale)
        causal = key_pos[None, :] <= t[:, None]
        score = jnp.where(causal[None], score, neg)
        _, sel = lax.top_k(score, topk)
        kg = k[bidx, sel]
        vg = v[bidx, sel]
        logits = jnp.einsum('bqhd,bqkd->bhqk', qb, kg).astype(jnp.float32) * scale
        valid = (sel <= t[None, :, None])[:, None]
        p = jax.nn.softmax(jnp.where(valid, logits, neg), axis=-1)
        return jnp.einsum('bhqk,bqkd->bqhd', p.astype(vg.dtype), vg)

    offs = jnp.arange(nb, dtype=jnp.int32) * BLOCK
    out = lax.map(blk, (to_blocks(q, nb), to_blocks(qi, nb), to_blocks(wi, nb), offs))
    return from_blocks(out)


def retention(q, k, v):
    B, S, H, d = q.shape
    dv = v.shape[-1]
    C = BLOCK
    nc = S // C
    log_g = jnp.log(1.0 - 2.0 ** (-5.0 - jnp.arange(H, dtype=jnp.float32)))
    n = jnp.arange(C, dtype=jnp.float32)
    diff = n[:, None] - n[None, :]
    Dm = jnp.where(diff[None] >= 0, jnp.exp(diff[None] * log_g[:, None, None]), 0.0)
    xi = jnp.exp((n + 1.0)[None, :] * log_g[:, None])
    zeta = jnp.exp((C - 1.0 - n)[None, :] * log_g[:, None])
    gC = jnp.exp(C * log_g)

    def chunks(t):
        return t.reshape(B, nc, C, H, t.shape[-1]).transpose(1, 0, 3, 2, 4)

    def step(R, inp):
        qc, kc, vc = inp
        inner = jnp.einsum('bhnm,bhme->bhne', jnp.einsum('bhnd,bhmd->bhnm', qc, kc) * Dm[None], vc)
        cross = jnp.einsum('bhnd,bhde->bhne', qc, R) * xi[None, :, :, None]
        R = R * gC[None, :, None, None] + jnp.einsum('bhmd,bhme->bhde', kc * zeta[None, :, :, None], vc)
        return R, inner + cross

    R0 = jnp.zeros((B, H, d, dv), jnp.float32)
    _, ys = lax.scan(step, R0, (chunks(q), chunks(k), chunks(v)))
    return ys.transpose(1, 0, 3, 2, 4).reshape(B, S, H, dv)


def head_groupnorm(r, g):
    mu = jnp.mean(r, axis=-1, keepdims=True)
    var = jnp.mean(jnp.square(r - mu), axis=-1, keepdims=True)
    y = (r - mu) * lax.rsqrt(var + EPS)
    return y.reshape(r.shape[0], r.shape[1], -1) * g.astype(jnp.float32)


def sliding_window_gqa(q, k, v, sinks):
    B, S, HQ, d = q.shape
    HKV = k.shape[2]
    G = HQ // HKV
    W = SWA_WINDOW
    nb = S // W
    neg = jnp.finfo(jnp.float32).min
    qb = q.reshape(B, nb, W, HKV, G, d)
    kb = k.reshape(B, nb, W, HKV, d)
    vb = v.reshape(B, nb, W, HKV, d)
    shift = lambda t: jnp.concatenate([jnp.zeros_like(t[:, :1]), t[:, :-1]], axis=1)
    kband = jnp.concatenate([shift(kb), kb], axis=2)
    vband = jnp.concatenate([shift(vb), vb], axis=2)
    logits = jnp.einsum('bnqgrd,bnkgd->bngrqk', qb, kband).astype(jnp.float32) * (d ** -0.5)
    i = jnp.arange(W)[:, None]
    j = jnp.arange(2 * W)[None, :]
    in_band = (j > i) & (j <= i + W)
    kpos = jnp.arange(nb)[:, None, None] * W - W + j[None]
    mask = in_band[None] & (kpos >= 0)
    logits = jnp.where(mask[None, :, None, None], logits, neg)
    sink = sinks.astype(jnp.float32).reshape(HKV, G)[None, None, :, :, None, None]
    m = jnp.maximum(jnp.max(logits, axis=-1, keepdims=True), sink)
    p = jnp.exp(logits - m)
    denom = jnp.sum(p, axis=-1, keepdims=True) + jnp.exp(sink - m)
    out = jnp.einsum('bngrqk,bnkgd->bnqgrd', (p / denom).astype(vband.dtype), vband)
    return out.reshape(B, S, HQ * d)


def causal_dwconv(u, w, b):
    C = u.shape[-1]
    out = lax.conv_general_dilated(u, w.astype(u.dtype)[:, None, :], window_strides=(1,),
                                   padding=[(CONV_WIDTH - 1, 0)],
                                   dimension_numbers=('NWC', 'WIO', 'NWC'),
                                   feature_group_count=C)
    return out + b.astype(u.dtype)


def setup_inputs(seed: int = 0) -> dict:
    key = jax.random.key(seed)
    ks = jax.random.split(key, 14)
    nrm = lambda k, shape: jax.random.normal(k, shape, jnp.float32)
    return {
        "x": nrm(ks[0], (BATCH, SEQ, D_MODEL)),
        "attn_norm": 1.0 + 0.02 * nrm(ks[1], (DEPTH, D_MODEL)),
        "w_in": nrm(ks[2], (DEPTH, D_MODEL, N_IN)) * D_MODEL ** -0.5,
        "forget_bias": 3.0 + 0.1 * nrm(ks[3], (DEPTH, FOX_HEADS)),
        "ret_norm": 1.0 + 0.02 * nrm(ks[4], (DEPTH, RET_HEADS * HEAD_DIM)),
        "attn_sinks": 0.1 * nrm(ks[5], (DEPTH, SWA_Q_HEADS)),
        "w_branch": nrm(ks[6], (DEPTH, N_BRANCH, BRANCH_WIDTH, D_MODEL)) * BRANCH_WIDTH ** -0.5,
        "w_out": nrm(ks[7], (DEPTH, D_MODEL, D_MODEL)) * D_MODEL ** -0.5,
        "ffn_norm": 1.0 + 0.02 * nrm(ks[8], (DEPTH, D_MODEL)),
        "w_up": nrm(ks[9], (DEPTH, D_MODEL, 2 * D_FF)) * D_MODEL ** -0.5,
        "conv_w": nrm(ks[10], (DEPTH, CONV_WIDTH, 2 * D_FF)) * CONV_WIDTH ** -0.5,
        "conv_b": 0.01 * nrm(ks[11], (DEPTH, 2 * D_FF)),
        "w_down": nrm(ks[12], (DEPTH, D_FF, D_MODEL)) * D_FF ** -0.5,
        "final_norm": 1.0 + 0.02 * nrm(ks[13], (D_MODEL,)),
    }


def reference(x, attn_norm, w_in, forget_bias, ret_norm, attn_sinks, w_branch, w_out,
              ffn_norm, w_up, conv_w, conv_b, w_down, final_norm):
    B, S, D = x.shape
    pos = jnp.arange(S, dtype=jnp.float32)
    split_at = np.cumsum(COL_SIZES)[:-1].tolist()
    heads = lambda t, n: t.reshape(B, S, n, -1)
    for l in range(DEPTH):
        h = rmsnorm(x, attn_norm[l])
        z = h @ w_in[l]
        (qA, kA, vA, fA, qB, kB, vB, qI, kI, wI,
         qC, kC, vC, gC, qD, kD, vD, gates) = jnp.split(z, split_at, axis=-1)

        yA = forgetting_attention(heads(qA, FOX_HEADS), heads(kA, FOX_HEADS), heads(vA, FOX_HEADS),
                                  fA, forget_bias[l])
        yB = dsa_attention(rope(heads(qB, DSA_HEADS), pos), rope(kB[:, :, None], pos)[:, :, 0], vB,
                           rope(heads(qI, IDX_HEADS), pos), rope(kI[:, :, None], pos)[:, :, 0], wI)
        r = retention(rope(heads(qC, RET_HEADS), pos).astype(jnp.float32),
                      (rope(heads(kC, RET_HEADS), pos) * HEAD_DIM ** -0.5).astype(jnp.float32),
                      heads(vC, RET_HEADS).astype(jnp.float32))
        yC = (head_groupnorm(r, ret_norm[l]) * jax.nn.silu(gC.astype(jnp.float32))).astype(x.dtype)
        yD = sliding_window_gqa(rope(heads(qD, SWA_Q_HEADS), pos), rope(heads(kD, SWA_KV_HEADS), pos),
                                heads(vD, SWA_KV_HEADS), attn_sinks[l])

        ys = jnp.stack([yA, yB, yC, yD], axis=2)
        branch = jnp.einsum('bsnc,ncd->bsnd', ys, w_branch[l])
        g = jax.nn.sigmoid(gates.reshape(B, S, N_BRANCH, D))
        merged = jnp.sum(g * branch, axis=2)
        x = x + merged @ w_out[l]

        h = rmsnorm(x, ffn_norm[l])
        u = causal_dwconv(h @ w_up[l], conv_w[l], conv_b[l])
        a, b = jnp.split(u, 2, axis=-1)
        x = x + (jax.nn.silu(a) * b) @ w_down[l]
    return rmsnorm(x, final_norm)
```

```python
import math
import os
from contextlib import ExitStack

import numpy as np
import concourse.bass as bass
import concourse.mybir as mybir
from concourse.bass_utils import run_bass_kernel_spmd

F32 = mybir.dt.float32
BF16 = mybir.dt.bfloat16
I32 = mybir.dt.int32
ALU = mybir.AluOpType
AF = mybir.ActivationFunctionType
AX = mybir.AxisListType

S = 2048
D = 1024
L = 2
NB = 16
DFF = 2816
EPS = 1e-6
NEG = -1.0e30
SEM_LIMIT = 24000


class Buf:
    __slots__ = ("name", "w", "r")

    def __init__(self, name=""):
        self.name = name
        self.w = None
        self.r = []


def bufs(n, name=""):
    return [Buf(f"{name}{i}") for i in range(n)]


class Prog:
    ENGS = ("pe", "dve", "act", "pool", "sp")

    def __init__(self, nc, stack, nsems=40):
        self.nc = nc
        self.q = {e: [] for e in self.ENGS}
        self.sempool = [stack.enter_context(nc.semaphore(f"s{i}")) for i in range(nsems)]
        self.nsem = 0
        self.cur = {}
        self.seen = {e: {} for e in self.ENGS}
        self.simq = {e: [] for e in self.ENGS}
        self.ninst = 0
        self.ninst_fg = 0
        self.bg = None
        self.bg_ratio = 0.0
        self.bg_steps = 0
        self._bg_acc = 0.0
        self._in_bg = False

    def start_bg(self, gen, ratio):
        self.bg, self.bg_ratio, self._bg_acc, self.bg_steps = gen, ratio, 0.0, 0
        self.bg_cost = 0.0
        self.bg_mark = 0

    def _bg_step(self):
        self._in_bg = True
        w = 0.0
        try:
            w = next(self.bg) or 0.0
            self.bg_steps += 1
            self.bg_cost += w
        except StopIteration:
            self.bg = None
        finally:
            self._in_bg = False
        return w

    def _tick(self):
        if self._in_bg:
            return
        self.ninst_fg += 1
        if self.bg is None:
            return
        self._bg_acc += self.bg_ratio
        while self._bg_acc > 0.0 and self.bg is not None:
            self._bg_acc -= max(self._bg_step(), 1e-3)

    def drain_bg(self, until_mark=None):
        while self.bg is not None and (until_mark is None or self.bg_mark < until_mark):
            self._bg_step()

    def _stream(self, name):
        if name not in self.cur:
            self.cur[name] = [self._newsem(), 0]
        return self.cur[name]

    def _newsem(self):
        i = self.nsem
        self.nsem += 1
        assert i < len(self.sempool), "out of semaphores"
        return i

    def _bump(self, name, by):
        st = self._stream(name)
        st[1] += by
        tok = (st[0], st[1])
        if st[1] >= SEM_LIMIT:
            self.cur[name] = [self._newsem(), 0]
        return tok

    def _peek(self, name, by):
        st = self._stream(name)
        return (st[0], st[1] + by)

    def _deps(self, eng, reads, writes, is_dma):
        need = {}

        def add(tok, kind):
            if tok is None:
                return
            e2, si, v = tok
            if (not is_dma) and e2 == eng and eng != "pool" and kind != "raw":
                return
            if need.get(si, 0) < v:
                need[si] = v
        for b in reads:
            add(b.w, "raw")
        for b in writes:
            add(b.w, "waw")
            for t in b.r:
                add(t, "war")
        out = []
        seen = self.seen[eng]
        for si, v in need.items():
            if seen.get(si, 0) < v:
                seen[si] = v
                out.append((si, v))
        return out

    def _commit(self, tok3, reads, writes):
        for b in reads:
            b.r.append(tok3)
            if len(b.r) > 48:
                mx = {}
                for (e, si, v) in b.r:
                    if si not in mx or mx[si][2] < v:
                        mx[si] = (e, si, v)
                b.r = list(mx.values())
        for b in writes:
            b.w = tok3
            b.r = []

    def op(self, eng, fn, reads=(), writes=(), inc=True):
        waits = self._deps(eng, reads, writes, False)
        if inc:
            si, v = self._bump(eng, 1)
        else:
            si, v = self._peek(eng, 1)
        sem = self.sempool[si]
        sems = self.sempool

        def emit(e, fn=fn, waits=waits, inc=inc, sem=sem):
            for (wi, wv) in waits:
                e.wait_ge(sems[wi], wv)
            ins = fn(e)
            if inc:
                ins.then_inc(sem, 1)
        self.q[eng].append(emit)
        self.simq[eng].append((tuple(waits), (si, 1) if inc else None, len(self.simq[eng])))
        self._commit((eng, si, v), reads, writes)
        self.ninst += 1
        self._tick()

    DMA_SLOTS = {"x": 10, "w": 8}

    def dma(self, eng, out, in_, reads=(), writes=(), stream="x", **kw):
        if not hasattr(self, "dstreams"):
            self.dstreams = {}
        if stream not in self.dstreams:
            k = self.DMA_SLOTS.get(stream, 4)
            self.dstreams[stream] = {"sems": [[self._newsem(), 0] for _ in range(k)], "next": 0}
        ds = self.dstreams[stream]
        slot = ds["next"]
        ds["next"] = (slot + 1) % len(ds["sems"])
        ent = ds["sems"][slot]
        if ent[1] + 16 > SEM_LIMIT:
            prev = tuple(ent)
            ent[0], ent[1] = self._newsem(), 0
        else:
            prev = tuple(ent)
        waits = self._deps(eng, reads, writes, True)
        if prev[1] > 0 and self.seen[eng].get(prev[0], 0) < prev[1]:
            self.seen[eng][prev[0]] = prev[1]
            waits.append(prev)
        ent[1] += 16
        si, v = ent[0], ent[1]
        sem = self.sempool[si]
        sems = self.sempool

        def emit(e, waits=waits, sem=sem, out=out, in_=in_, kw=kw):
            for (wi, wv) in waits:
                e.wait_ge(sems[wi], wv)
            e.dma_start(out=out, in_=in_, **kw).then_inc(sem, 16)
        self.q[eng].append(emit)
        self.simq[eng].append((tuple(waits), (si, 16), len(self.simq[eng])))
        self._commit(("dma", si, v), reads, writes)
        self.ninst += 1
        self._tick()

    def wait_all(self, eng, bl):
        waits = self._deps(eng, bl, (), True)
        sems = self.sempool

        def emit(e, waits=waits):
            for (wi, wv) in waits:
                e.wait_ge(sems[wi], wv)
        self.q[eng].append(emit)

    def check_deadlock(self):
        sem = {}
        pos = {e: 0 for e in self.ENGS}
        progress = True
        while progress:
            progress = False
            for e in self.ENGS:
                q = self.simq[e]
                while pos[e] < len(q):
                    waits, inc, _ = q[pos[e]]
                    if any(sem.get(si, 0) < v for (si, v) in waits):
                        break
                    if inc is not None:
                        sem[inc[0]] = sem.get(inc[0], 0) + inc[1]
                    pos[e] += 1
                    progress = True
        stuck = {e: (pos[e], len(self.simq[e])) for e in self.ENGS if pos[e] < len(self.simq[e])}
        if stuck:
            msg = []
            for e, (p, n) in stuck.items():
                waits, inc, _ = self.simq[e][p]
                msg.append(f"{e}: stuck at {p}/{n} waits={[(si, v, sem.get(si, 0)) for (si, v) in waits if sem.get(si, 0) < v]}")
            raise RuntimeError("DEADLOCK in emitted program: " + " | ".join(msg))

    def finish(self):
        self.check_deadlock()
        nc = self.nc
        q = self.q
        with nc.Block() as block:
            @block.tensor
            def _(e):
                for f in q["pe"]:
                    f(e)

            @block.vector
            def _(e):
                for f in q["dve"]:
                    f(e)

            @block.scalar
            def _(e):
                for f in q["act"]:
                    f(e)

            @block.gpsimd
            def _(e):
                for f in q["pool"]:
                    f(e)

            @block.sync
            def _(e):
                for f in q["sp"]:
                    f(e)


COL_SIZES = (512, 512, 512, 8, 512, 64, 64, 256, 64, 4, 512, 512, 512, 512, 512, 128, 128, 4096)
COL_NAMES = ("qA", "kA", "vA", "fA", "qB", "kB", "vB", "qI", "kI", "wI", "qC", "kC", "vC", "gC",
             "qD", "kD", "vD", "gates")
COL_OFF = {}
_o = 0
for _n, _s in zip(COL_NAMES, COL_SIZES):
    COL_OFF[_n] = (_o, _s)
    _o += _s
N_IN = _o


def _cols(name):
    o, s = COL_OFF[name]
    return np.arange(o, o + s)


def _swap(cols):
    c = cols.reshape(-1, 64)
    return np.concatenate([c[:, 32:], c[:, :32]], axis=1).reshape(-1)


def _dup(cols64):
    return np.concatenate([cols64, cols64])


ZCOL = sum(COL_SIZES)


def _padlo(cols64):
    return np.concatenate([cols64, np.full(64, ZCOL)])


def _padhi(cols64):
    return np.concatenate([np.full(64, ZCOL), cols64])


def fm_chunk_cols():
    ch = []

    def add(name, cols):
        cols = np.asarray(cols)
        assert cols.size % 128 == 0
        for i in range(cols.size // 128):
            ch.append((f"{name}{i}", cols[i * 128:(i + 1) * 128]))
    add("qC", _cols("qC")); add("qCs", _swap(_cols("qC")))
    add("kC", _cols("kC")); add("kCs", _swap(_cols("kC")))
    add("qB", _cols("qB")); add("qBs", _swap(_cols("qB")))
    add("kBe", _padlo(_cols("kB"))); add("kBes", _padlo(_swap(_cols("kB"))))
    add("kBo", _padhi(_cols("kB"))); add("kBos", _padhi(_swap(_cols("kB"))))
    add("qI", _cols("qI")); add("qIs", _swap(_cols("qI")))
    add("kI", _dup(_cols("kI"))); add("kIs", _dup(_swap(_cols("kI"))))
    add("qA", _cols("qA")); add("kA", _cols("kA"))
    add("qD", _cols("qD")); add("qDs", _swap(_cols("qD")))
    kd = _cols("kD")
    kds = _swap(kd)
    add("kD", np.concatenate([_padlo(kd[:64]), _padhi(kd[:64]), _padlo(kd[64:]), _padhi(kd[64:])]))
    add("kDs", np.concatenate([_padlo(kds[:64]), _padhi(kds[:64]), _padlo(kds[64:]), _padhi(kds[64:])]))
    add("g", _cols("gates"))
    return ch


FM = fm_chunk_cols()
FM_IDX = {n: i for i, (n, _) in enumerate(FM)}
NFM = len(FM)

TM_GROUPS = [("vC", _cols("vC")), ("gC", _cols("gC")),
             ("vB", _cols("vB")), ("wI", _cols("wI")),
             ("vA", _cols("vA")), ("fA", _cols("fA")), ("vD", _cols("vD"))]
TM_OFF = {}
_o = 0
for _n, _c in TM_GROUPS:
    TM_OFF[_n] = (_o, _c.size)
    _o += _c.size
NTM = _o

LOG_G = [math.log(1.0 - 2.0 ** (-5.0 - h)) for h in range(8)]


def build_program(nl=L, dbg=()):
    stats = {}
    _build(nl, dbg, [0.2], stats)
    fg, steps = stats.get(0, (1, 0))
    ratio = steps / max(1.0, 0.70 * fg)
    if os.environ.get("KDEBUG"):
        print("bg calibration: fg ops", fg, "bg steps", steps, "ratio", ratio)
    return _build(nl, dbg, [ratio], {})


def _build(nl, dbg, bg_ratio, bg_stats):
    nc = bass.Bass("TRN2", target_bir_lowering=False)
    dram = lambda name, shape, dt, kind: nc.dram_tensor(name, list(shape), dt, kind=kind).ap()
    x_in = dram("x", [S, D], F32, "ExternalInput")
    wfm = dram("wfm", [L, NFM, 128, 8, 128], F32, "ExternalInput")
    wtm = dram("wtm", [L, 128, 8, NTM], F32, "ExternalInput")
    wbr = dram("wbr", [L, 4, 8, 128, 4, 128], F32, "ExternalInput")
    wout = dram("wout", [L, 128, 8, 1024], F32, "ExternalInput")
    wup = dram("wup", [L, 44, 128, 8, 128], F32, "ExternalInput")
    wdn = dram("wdn", [L, 128, 22, 1024], F32, "ExternalInput")
    gat_d = dram("gattn", [128, L * 8], F32, "ExternalInput")
    gff_d = dram("gffn", [128, L * 8], F32, "ExternalInput")
    gfin_d = dram("gfin", [1, 1024], F32, "ExternalInput")
    fbias_d = dram("fbias", [1, L * 8], F32, "ExternalInput")
    retn_d = dram("retn", [L, 1, 512], F32, "ExternalInput")
    sinks_d = dram("sinks", [1, L * 8], F32, "ExternalInput")
    convp_d = dram("convp", [128, L * 44 * 4], F32, "ExternalInput")
    out_d = dram("out", [S, D], F32, "ExternalOutput")
    xs_d = dram("xs", [S, D], F32, "Internal")
    ysT_d = dram("ysT", [4, 128, 4, S], BF16, "Internal")
    MT_d = dram("MTd", [4, 128, 16, 512], BF16, "Internal")
    dbg_d = {}
    for name, shape, dt in dbg:
        dbg_d[name] = dram("dbg_" + name, shape, dt, "ExternalOutput")

    with ExitStack() as st:
        P = Prog(nc, st)
        sbt = lambda name, shape, dt: st.enter_context(nc.sbuf_tensor(name, list(shape), dt))
        pst = lambda name, shape, dt: st.enter_context(nc.psum_tensor(name, list(shape), dt))

        ident = sbt("ident", [128, 128], BF16); b_ident = Buf()
        Mc = sbt("Mc", [128, 128], BF16); Mp = sbt("Mp", [128, 128], BF16); b_mask = Buf()
        NegMp = sbt("NegMp", [128, 128], BF16)
        TriF = sbt("TriF", [128, 128], F32); OnesF = sbt("OnesF", [128, 128], F32); SelL = sbt("SelL", [128, 128], F32)
        b_cf = Buf()
        ropeC = sbt("ropeC", [128, S], BF16); ropeS = sbt("ropeS", [128, S], BF16); b_rope = Buf()
        gat = sbt("gat", [128, L * 8], F32); gff = sbt("gff", [128, L * 8], F32)
        fbias = sbt("fbiasb", [128, L * 8], F32); esink = sbt("esink", [128, L * 8], F32)
        convp = sbt("convpb", [128, L * 44 * 4], F32)
        b_par = Buf()
        retn = sbt("retnb", [128, 512], F32); b_retn = Buf()
        small = sbt("small", [128, 64], F32)
        b_small = bufs(64, "sm")
        hT = sbt("hT", [128, 8, S], BF16); b_hT = bufs(NB, "hT")
        RA = sbt("RA", [128, 32896], BF16)
        RB = sbt("RB", [128, 24576], BF16)
        wst = [sbt(f"wst{i}", [128, 8, 128], BF16) for i in range(4)]; b_wst = bufs(4, "wst")
        wtmt = sbt("wtmt", [128, 8, 512], BF16); b_wtmh = bufs(2, "wtm")
        PT = [sbt(f"PT{i}", [128, 512], BF16) for i in range(3)]; b_PT = bufs(3, "PT")
        ET = [sbt(f"ET{i}", [128, 512], BF16) for i in range(2)]; b_ET = bufs(2, "ET")
        tmp = [sbt(f"tmp{i}", [128, 512], F32) for i in range(4)]; b_tmp = bufs(4, "tmp")
        xb = [sbt(f"xb{i}", [128, 1024], F32) for i in range(2)]; b_xb = bufs(2, "xb")
        xn = sbt("xn", [128, 1024], BF16); b_xn = Buf()
        Ft = sbt("Ft", [128, 16, 8], F32); b_F = Buf()
        lt = sbt("lt", [128, 16, 8], F32); b_lt = Buf()
        Fend = sbt("Fend", [128, 4, 8], F32); b_Fend = Buf()
        biasA = sbt("biasA", [128, 4, 16, 8], F32); b_biasA = Buf()
        wIt = sbt("wIt", [128, 16, 4], F32); b_wI = Buf()
        m8 = sbt("m8", [128, 8], F32); b_m8 = Buf()
        dummy = sbt("dummyt", [128, 2], F32); b_dummy = Buf()
        itmp = [sbt(f"itmp{i}", [128, 512], F32) for i in range(2)]; b_itmp = bufs(2, "itmp")
        dens = sbt("dens", [128, 4, 8], F32)
        negc = sbt("negc", [128, 16], F32); b_negc = Buf()
        b_mtd = bufs(4, "mtd")
        lgt = sbt("lgt", [128, 40], F32); b_lgt = Buf()
        NMMv = [3]
        PSt = [pst(f"ps{i}", [128, 512], F32) for i in range(8)]; b_PS = bufs(8, "ps")

        qT = RA[:, 0:8192].rearrange("p (c t) -> p c t", c=4)
        kT = RA[:, 8192:16384].rearrange("p (c t) -> p c t", c=4)
        vaug = RA[:, 16384:16384 + 8320]
        aux = RA[:, 24704:24704 + 8192]
        b_qT = [bufs(4) for _ in range(4)]
        b_kT = [bufs(4) for _ in range(4)]
        b_v = bufs(NB, "v")
        b_aux = bufs(16, "aux")
        RA_P2 = [b for l_ in b_qT for b in l_] + [b for l_ in b_kT for b in l_] + b_v + b_aux
        yT = RA[:, 0:32768].rearrange("p (n c t) -> p n c t", n=4, c=4)
        b_yT = [bufs(4) for _ in range(4)]
        RA_P3 = [b for l_ in b_yT for b in l_]
        actT = RA[:, 0:22528].rearrange("p (c t) -> p c t", c=22)
        b_act = [bufs(2) for _ in range(22)]
        ubuf = RA[:, 22528:22528 + 4 * 1032].bitcast(F32).rearrange("p (k t) -> p k t", k=4)
        b_ub = bufs(4, "ub")
        b_ubh = bufs(4, "ubh")
        RA_P5 = [b for l_ in b_act for b in l_] + b_ub + b_ubh
        aux2 = RB[:, 0:8192].rearrange("p (j t) -> p j t", j=16)
        b_aux2 = bufs(16, "aux2")
        acc = RB[:, 8192:12288].bitcast(F32)
        b_acc = bufs(4, "acc")
        rbuf = RB[:, 12288:16384].bitcast(F32).rearrange("p (i t) -> p i t", i=4)
        b_rbuf = bufs(4, "rbuf")
        ybuf = RB[:, 16384:20480].rearrange("p (k i t) -> p k i t", k=2, i=4)
        b_ybuf = [bufs(4) for _ in range(2)]
        ytc = RB[:, 20480:24576].rearrange("p (k c t) -> p k c t", k=2, c=4)
        b_ytc = bufs(2, "ytc")
        RB_P2 = b_aux2 + b_acc + b_rbuf + [b for l_ in b_ybuf for b in l_] + b_ytc
        mergedT = RB[:, 0:16384].rearrange("p (c t) -> p c t", c=8)
        b_mg = [bufs(4) for _ in range(8)]
        woutt = RB[:, 16384:24576].rearrange("p (c n) -> p c n", c=8)
        b_wout = Buf()
        RB_P3 = [b for l_ in b_mg for b in l_] + [b_wout]
        wdnt = RB[:, 0:22528].rearrange("p (c n) -> p c n", c=22)
        b_wdn = Buf()
        RB_P5 = [b_wdn]
        isc = [RB[:, k * 4096:(k + 1) * 4096].bitcast(F32) for k in range(3)]
        RB_INIT = bufs(3, "isc")

        b_xs = bufs(NB, "xs")
        b_ys = [bufs(4) for _ in range(4)]
        b_out = bufs(NB, "out")
        b_dbg = Buf()

        def ACT(out, in_, func, reads, writes, **kw):
            P.op("act", lambda e: e.activation(out=out, in_=in_, func=func, **kw), reads, writes)

        def TT(eng, out, in0, in1, op, reads, writes):
            P.op(eng, lambda e: e.tensor_tensor(out=out, in0=in0, in1=in1, op=op), reads, writes)

        def TS(eng, out, in0, s1, s2, op0, op1, reads, writes, **kw):
            if op1 is None:
                P.op(eng, lambda e: e.tensor_scalar(out=out, in0=in0, scalar1=s1, scalar2=None, op0=op0, **kw), reads, writes)
            else:
                P.op(eng, lambda e: e.tensor_scalar(out=out, in0=in0, scalar1=s1, scalar2=s2, op0=op0, op1=op1, **kw), reads, writes)

        def STT(out, in0, scalar, in1, op0, op1, reads, writes):
            P.op("dve", lambda e: e.scalar_tensor_tensor(out=out, in0=in0, scalar=scalar, in1=in1, op0=op0, op1=op1), reads, writes)

        def CP(eng, out, in_, reads, writes):
            if eng == "act":
                P.op("act", lambda e: e.activation(out=out, in_=in_, func=AF.Copy), reads, writes)
            else:
                P.op(eng, lambda e: e.tensor_copy(out=out, in_=in_), reads, writes)

        def MM(out, lhsT, rhs, start, stop, reads, writes, inc=None):
            if inc is None:
                inc = bool(stop)
            P.op("pe", lambda e: e.matmul(out, lhsT=lhsT, rhs=rhs, start=start, stop=stop), reads, writes, inc=inc)

        def TR(out, in_, reads, writes, inc=True):
            P.op("pe", lambda e: e.transpose(out, in_, ident[:]), list(reads) + [b_ident], writes, inc=inc)

        def fence(old, new):
            P.op("pool", lambda e: e.memset(dummy[:, 0:1], 0.0), reads=[], writes=list(old) + list(new) + [b_dummy])

        rr = {"mm": 0, "tmp": 0, "PT": 0, "ET": 0, "wst": 0, "xb": 0, "sm": 0, "itmp": 0, "acc": 0}

        def nxt(kind, n):
            v = rr[kind]
            rr[kind] = (v + 1) % n
            return v

        def smcol(n=1):
            v = rr["sm"]
            if v + n > 64:
                v = 0
            rr["sm"] = (v + n) % 64
            return small[:, v:v + n], b_small[v:v + n]

        def dump(name, src_ap, reads):
            if name in dbg_d:
                P.dma("sp", dbg_d[name], src_ap, reads=reads, writes=[b_dbg])

        P.op("pool", lambda e: e.memset(dummy[:], 0.0), writes=[b_dummy])
        P.op("pool", lambda e: e.memset(negc[:, 0:8], -1.0), writes=[b_negc])
        P.op("pool", lambda e: e.memset(negc[:, 8:16], -0.5), writes=[b_negc])
        for ch_ in range(4):
            for half_ in range(2):
                lg_ = LOG_G[2 * ch_ + half_]
                ps_ = slice(64 * half_, 64 * half_ + 64)
                P.op("pool", lambda e, ps_=ps_, ch_=ch_, lg_=lg_: e.memset(lgt[ps_, ch_:ch_ + 1], lg_), writes=[b_lgt])
                P.op("pool", lambda e, ps_=ps_, ch_=ch_, lg_=lg_: e.memset(lgt[ps_, 4 + ch_:5 + ch_], -lg_), writes=[b_lgt])
                for tc_ in range(4):
                    P.op("pool", lambda e, ps_=ps_, ch_=ch_, lg_=lg_, tc_=tc_: e.memset(lgt[ps_, 8 + ch_ * 4 + tc_:9 + ch_ * 4 + tc_], lg_ * 512.0 * tc_), writes=[b_lgt])
                    P.op("pool", lambda e, ps_=ps_, ch_=ch_, lg_=lg_, tc_=tc_: e.memset(lgt[ps_, 24 + ch_ * 4 + tc_:25 + ch_ * 4 + tc_], -lg_ * 512.0 * tc_), writes=[b_lgt])
        i0 = isc[0][:, 0:128]
        P.op("pool", lambda e: e.memset(i0, 0.0), writes=[RB_INIT[0]])
        P.op("pool", lambda e: e.affine_select(out=i0, in_=i0, pattern=[[-1, 128]], compare_op=ALU.not_equal, fill=1.0,
                                               base=0, channel_multiplier=1), reads=[RB_INIT[0]], writes=[RB_INIT[0]])
        CP("dve", ident[:], i0, [RB_INIT[0]], [b_ident])
        i1 = isc[0][:, 128:256]
        P.op("pool", lambda e: e.memset(i1, 1.0), writes=[RB_INIT[0]])
        P.op("pool", lambda e: e.affine_select(out=TriF[:], in_=i1, pattern=[[1, 128]], compare_op=ALU.is_ge, fill=0.0,
                                               base=0, channel_multiplier=-1), reads=[RB_INIT[0]], writes=[b_cf])
        CP("dve", Mc[:], TriF[:], [b_cf], [b_mask])
        TS("dve", Mp[:], TriF[:], -1.0, 1.0, ALU.mult, ALU.add, [b_cf], [b_mask])
        TS("dve", NegMp[:], TriF[:], 30000.0, -30000.0, ALU.mult, ALU.add, [b_cf], [b_mask])
        P.op("pool", lambda e: e.memset(OnesF[:], 1.0), writes=[b_cf])
        i2 = isc[0][:, 256:384]
        P.op("pool", lambda e: e.memset(i2, 0.0), writes=[RB_INIT[0]])
        P.op("pool", lambda e: e.affine_select(out=SelL[:], in_=i2, pattern=[[0, 128]], compare_op=ALU.not_equal, fill=1.0,
                                               base=-127, channel_multiplier=1), reads=[RB_INIT[0]], writes=[b_cf])
        P.dma("sp", gat[:], gat_d, writes=[b_par])
        P.dma("sp", gff[:], gff_d, writes=[b_par])
        P.dma("sp", fbias[:], fbias_d.partition_broadcast(128), writes=[b_par])
        P.dma("sp", esink[:], sinks_d.partition_broadcast(128), writes=[b_par])
        P.dma("sp", convp[:], convp_d, writes=[b_par])
        ACT(esink[:], esink[:], AF.Exp, [b_par], [b_par])
        bs0 = b_small[0:12]
        pf = small[:, 0:1]
        P.op("pool", lambda e: e.iota(pf, pattern=[[0, 1]], base=0, channel_multiplier=1, allow_small_or_imprecise_dtypes=True), writes=bs0)
        ki = small[:, 1:2].bitcast(I32)
        TS("dve", small[:, 2:3], pf, 1.0 / 64.0, 0.5 / 64.0 - 0.5, ALU.mult, ALU.add, bs0, bs0)
        CP("dve", ki, small[:, 2:3], bs0, bs0)
        CP("dve", small[:, 2:3], ki, bs0, bs0)
        STT(small[:, 3:4], small[:, 2:3], -64.0, pf, ALU.mult, ALU.add, bs0, bs0)
        TS("dve", small[:, 4:5], small[:, 3:4], 1.0 / 32.0, 0.5 / 32.0 - 0.5, ALU.mult, ALU.add, bs0, bs0)
        CP("dve", ki, small[:, 4:5], bs0, bs0)
        CP("dve", small[:, 4:5], ki, bs0, bs0)
        STT(small[:, 3:4], small[:, 4:5], -32.0, small[:, 3:4], ALU.mult, ALU.add, bs0, bs0)
        ACT(small[:, 5:6], small[:, 3:4], AF.Exp, bs0, bs0, scale=-2.0 * math.log(10000.0) / 64.0)
        TS("dve", small[:, 5:6], small[:, 5:6], 1.0 / (2.0 * math.pi), None, ALU.mult, None, bs0, bs0)
        TS("dve", small[:, 6:7], small[:, 4:5], 2.0, -1.0, ALU.mult, ALU.add, bs0, bs0)
        upos, kint, kf = isc[0], isc[1].bitcast(I32), isc[2]
        P.op("pool", lambda e: e.iota(upos, pattern=[[1, S]], base=0, channel_multiplier=0, allow_small_or_imprecise_dtypes=True),
             reads=[RB_INIT[0]], writes=[RB_INIT[0]])
        TS("dve", upos, upos, small[:, 5:6], None, ALU.mult, None, [RB_INIT[0]] + bs0, [RB_INIT[0]])
        SIN_SCALE = 6.2831845
        for which in range(2):
            if which == 1:
                TS("dve", upos, upos, 0.25, None, ALU.add, None, [RB_INIT[0]], [RB_INIT[0]])
            CP("dve", kint, upos, [RB_INIT[0]], [RB_INIT[1]])
            CP("dve", kf, kint, [RB_INIT[1]], [RB_INIT[2]])
            TT("dve", kf, upos, kf, ALU.subtract, [RB_INIT[0], RB_INIT[2]], [RB_INIT[2]])
            ACT(kf, kf, AF.Sin, [RB_INIT[2]], [RB_INIT[2]], scale=SIN_SCALE)
            if which == 0:
                TS("dve", ropeS[:], kf, small[:, 6:7], None, ALU.mult, None, [RB_INIT[2]] + bs0, [b_rope])
            else:
                CP("dve", ropeC[:], kf, [RB_INIT[2]], [b_rope])
        dump("ropeC", ropeC[:], [b_rope]); dump("ropeS", ropeS[:], [b_rope])
        fence(RB_INIT + bs0, RB_P2 + b_small)

        def load_fm(l, name):
            k = nxt("wst", 4)
            P.dma("pool", wst[k][:], wfm[l, FM_IDX[name]], writes=[b_wst[k]], stream="w")
            return wst[k], b_wst[k]

        def proj_fm_mm(wt, bw, tc):
            k = nxt("mm", NMMv[0])
            hb = b_hT[tc * 4:(tc + 1) * 4]
            for kc in range(8):
                MM(PSt[k][:], wt[:, kc, :], hT[:, kc, tc * 512:(tc + 1) * 512], kc == 0, kc == 7, [bw] + hb, [b_PS[k]])
            return k

        def unit_plain(l, name, dest_fn):
            def ld():
                return load_fm(l, name)

            def comp(hd):
                wt, bw = hd
                for tc in range(4):
                    k = proj_fm_mm(wt, bw, tc)
                    dst, dbuf = dest_fn(tc)
                    if isinstance(dst, tuple):
                        CP("act", dst[0][0:64], PSt[k][0:64, :], [b_PS[k]], [dbuf[0]])
                        CP("act", dst[1][64:128], PSt[k][64:128, :], [b_PS[k]], [dbuf[1]])
                    else:
                        CP("act", dst, PSt[k][:], [b_PS[k]], [dbuf])
            return ld, comp

        def unit_rope(l, name, sname, dest_fn, decay=None):
            def ld():
                return load_fm(l, name), load_fm(l, sname)

            def comp(hd):
                (wt, bw), (ws, bws) = hd
                for tc in range(4):
                    k1 = proj_fm_mm(wt, bw, tc)
                    k2 = proj_fm_mm(ws, bws, tc)
                    dst, dbuf = dest_fn(tc)
                    t1 = nxt("tmp", 4); t2 = nxt("tmp", 4)
                    sl = slice(tc * 512, (tc + 1) * 512)
                    CP("act", tmp[t1][:], PSt[k1][:], [b_PS[k1]], [b_tmp[t1]])
                    CP("act", tmp[t2][:], PSt[k2][:], [b_PS[k2]], [b_tmp[t2]])
                    rr["rope"] = (rr.get("rope", 0) + 1) % 3
                    en_ = "dve" if rr["rope"] == 0 else "pool"
                    TT(en_, tmp[t1][:], tmp[t1][:], ropeC[:, sl], ALU.mult, [b_tmp[t1], b_rope], [b_tmp[t1]])
                    TT(en_, tmp[t2][:], tmp[t2][:], ropeS[:, sl], ALU.mult, [b_tmp[t2], b_rope], [b_tmp[t2]])
                    if decay is None:
                        TT(en_, dst, tmp[t1][:], tmp[t2][:], ALU.add, [b_tmp[t1], b_tmp[t2]], [dbuf])
                    else:
                        ch_, sgn_, tposF, b_tpos = decay
                        o_ = 0 if sgn_ > 0 else 4
                        ob_ = 8 if sgn_ > 0 else 24
                        ke = nxt("ET", 2)
                        P.op("act", lambda e, ke=ke, ch_=ch_, o_=o_, ob_=ob_, tc=tc: e.activation(
                            out=ET[ke][:], in_=tposF, func=AF.Exp, scale=lgt[:, o_ + ch_:o_ + ch_ + 1],
                            bias=lgt[:, ob_ + ch_ * 4 + tc:ob_ + ch_ * 4 + tc + 1]), [b_tpos, b_lgt], [b_ET[ke]])
                        TT(en_, tmp[t1][:], tmp[t1][:], tmp[t2][:], ALU.add, [b_tmp[t1], b_tmp[t2]], [b_tmp[t1]])
                        if isinstance(dst, tuple):
                            TT(en_, dst[0][0:64], tmp[t1][0:64, :], ET[ke][0:64, :], ALU.mult, [b_tmp[t1], b_ET[ke]], [dbuf[0]])
                            TT(en_, dst[1][64:128], tmp[t1][64:128, :], ET[ke][64:128, :], ALU.mult, [b_tmp[t1], b_ET[ke]], [dbuf[1]])
                        else:
                            TT(en_, dst, tmp[t1][:], ET[ke][:], ALU.mult, [b_tmp[t1], b_ET[ke]], [dbuf])
            return ld, comp

        def run_units(units, first=None, depth=1):
            q_ = [first if first is not None else units[0][0]()]
            nxt_ld = 1
            for k, (ld, comp) in enumerate(units):
                while nxt_ld < len(units) and nxt_ld <= k + depth:
                    q_.append(units[nxt_ld][0]())
                    nxt_ld += 1
                comp(q_.pop(0))

        def load_tm(l, gname):
            o, n = TM_OFF[gname]
            P.dma("pool", wtmt[:, :, 0:n], wtm[l, :, :, o:o + n], writes=b_wtmh, stream="w")
            return n

        def proj_tm_mm(blk, n):
            k = nxt("mm", NMMv[0])
            for kc in range(8):
                MM(PSt[k][:, 0:n], hT[:, kc, blk * 128:(blk + 1) * 128], wtmt[:, kc, 0:n], kc == 0, kc == 7,
                   b_wtmh + [b_hT[blk]], [b_PS[k]])
            return k

        def resid_norm(l_next, blk, ps_halves, src, gtile, first=False, final=False):
            k = nxt("xb", 2)
            xt, bx = xb[k], b_xb[k]
            if first:
                P.dma("sp", xt[:], x_in[blk * 128:(blk + 1) * 128, :], writes=[bx])
            else:
                P.dma("sp", xt[:], xs_d[blk * 128:(blk + 1) * 128, :], reads=[b_xs[blk]], writes=[bx])
            if ps_halves is not None:
                for hf, kp in enumerate(ps_halves):
                    sl = slice(hf * 512, (hf + 1) * 512)
                    TT("dve", xt[:, sl], xt[:, sl], PSt[kp][:], ALU.add, [bx, b_PS[kp]], [bx])
                if not final:
                    P.dma("sp", xs_d[blk * 128:(blk + 1) * 128, :], xt[:], reads=[bx], writes=[b_xs[blk]])
            elif first:
                P.dma("sp", xs_d[blk * 128:(blk + 1) * 128, :], xt[:], reads=[bx], writes=[b_xs[blk]])
            ss, bss = smcol(2)
            ACT(xn[:], xt[:], AF.Square, [bx], [b_xn] + bss, accum_out=ss[:, 0:1])
            ACT(ss[:, 1:2], ss[:, 0:1], AF.Sqrt, bss, bss, scale=1.0 / D, bias=EPS)
            P.op("dve", lambda e: e.reciprocal(out=ss[:, 1:2], in_=ss[:, 1:2]), bss, bss)
            if final:
                TS("dve", xt[:], xt[:], ss[:, 1:2], None, ALU.mult, None, [bx] + bss, [bx])
                TT("pool", xt[:], xt[:], gfin_t, ALU.mult, [bx] + b_wtmh, [bx])
                P.dma("sp", out_d[blk * 128:(blk + 1) * 128, :], xt[:], reads=[bx], writes=[b_out[blk]])
                return
            TS("dve", xn[:], xt[:], ss[:, 1:2], None, ALU.mult, None, [bx] + bss, [b_xn])
            kp = nxt("mm", NMMv[0])
            pb = PSt[kp][:].bitcast(BF16)
            for kc in range(8):
                TR(pb[:, kc * 128:(kc + 1) * 128], xn[:, kc * 128:(kc + 1) * 128], [b_xn], [b_PS[kp]], inc=(kc == 7))
            g3 = gtile[:, l_next * 8:(l_next + 1) * 8].unsqueeze(2).to_broadcast([128, 8, 128])
            TT("dve", hT[:, :, blk * 128:(blk + 1) * 128], pb.rearrange("p (c t) -> p c t", c=8), g3, ALU.mult,
               [b_PS[kp], b_par], [b_hT[blk]])

        def attn(kind, l, n_branch, qsrc, ksrc, vsrc, vcols, finalize, chunk_prep=None, chunk_done=None, LA=1, corder=(0, 1, 2, 3)):
            tiles = []
            for c in corder:
                for h in range(8):
                    if kind == "D":
                        jlist = list(range(max(4 * c - 1, 0), 4 * c + 4))
                    else:
                        jlist = list(range(0, 4 * c + 4))
                    for j in jlist:
                        tiles.append((c, h, j, h == 0 and j == jlist[0], h == 7 and j == jlist[-1]))

            def emit_S(c, h, j):
                qa, qb = qsrc(h)
                ka, kb = ksrc(h)
                i0 = max(j, 4 * c)
                i1 = min(j + 1, 4 * c + 3) if kind == "D" else 4 * c + 3
                n = (i1 - i0 + 1) * 128
                ks = nxt("mm", 3)
                if kind == "B":
                    MM(PSt[ks][:, 0:n], ka[:, j * 128:(j + 1) * 128], qa[:, i0 * 128:(i1 + 1) * 128], True, False,
                       [kb[j // 4], qb[c]], [b_PS[ks]])
                    offb = (i0 - 4 * c) * 128
                    MM(PSt[ks][:, 0:n], ident[:], aux2[:, j, offb:offb + n], False, True, [b_ident, b_aux2[j]], [b_PS[ks]])
                else:
                    MM(PSt[ks][:, 0:n], ka[:, j * 128:(j + 1) * 128], qa[:, i0 * 128:(i1 + 1) * 128], True, True,
                       [kb[j // 4], qb[c]], [b_PS[ks]])
                kp = nxt("PT", 3)
                pt, bpt = PT[kp], b_PT[kp]
                if kind == "A":
                    ACT(pt[:, 0:n], PSt[ks][:, 0:n], AF.Exp, [b_PS[ks], b_biasA], [bpt],
                        bias=biasA[:, c, j, h:h + 1], scale=0.125)
                    if j >= 4 * c:
                        TT("pool", pt[:, 0:128], pt[:, 0:128], Mc[:], ALU.mult, [bpt, b_mask], [bpt])
                elif kind == "B":
                    off = (i0 - 4 * c) * 128
                    ACT(pt[:, 0:n], PSt[ks][:, 0:n], AF.Exp, [b_PS[ks]], [bpt], scale=0.125)
                elif kind == "C":
                    P.op("act", lambda e, ks=ks, n=n, pt=pt: e.activation(out=pt[:, 0:n], in_=PSt[ks][:, 0:n], func=AF.Copy, scale=0.125),
                         [b_PS[ks]], [bpt])
                    if j >= 4 * c:
                        TT("pool", pt[:, 0:128], pt[:, 0:128], Mc[:], ALU.mult, [bpt, b_mask], [bpt])
                else:
                    ACT(pt[:, 0:n], PSt[ks][:, 0:n], AF.Exp, [b_PS[ks]], [bpt], scale=0.125)
                    for i in range(i0, i1 + 1):
                        o = (i - i0) * 128
                        mk = Mc if i == j else Mp
                        TT("pool", pt[:, o:o + 128], pt[:, o:o + 128], mk[:], ALU.mult, [bpt, b_mask], [bpt])
                return (c, h, j, i0, i1, pt, bpt)

            def emit_PV(info, last_of_chunk):
                c, h, j, i0, i1, pt, bpt = info
                for i in range(i0, i1 + 1):
                    va, vb = vsrc(h, j)
                    ko = 4 + (i - 4 * c)
                    first = (j == max(i - 1, 0)) if kind == "D" else (j == 0)
                    o = (i - i0) * 128
                    MM(PSt[ko][:, 0:vcols], pt[:, o:o + 128], va, first, j == i, [bpt, vb], [b_PS[ko]],
                       inc=(j == i or i == i1))
                    if j == i:
                        finalize(c, h, i, ko)
                if last_of_chunk and chunk_done is not None:
                    part2 = chunk_done(c)
                    if part2 is not None:
                        deferred.append([6, part2])

            def tick_deferred(flush=False):
                for d in list(deferred):
                    d[0] -= 1
                    if d[0] <= 0 or flush:
                        deferred.remove(d)
                        d[1]()

            deferred = []
            pend = []
            for (c, h, j, first_of_chunk, last_of_chunk) in tiles:
                if first_of_chunk and chunk_prep is not None:
                    chunk_prep(c)
                pend.append((emit_S(c, h, j), last_of_chunk))
                if len(pend) > LA:
                    emit_PV(*pend.pop(0))
                    tick_deferred()
            while pend:
                emit_PV(*pend.pop(0))
            tick_deferred(flush=True)

        def y_transpose_store(n_branch, c, par):
            k2 = c % 2
            for ii in range(4):
                kp = nxt("mm", NMMv[0])
                pb = PSt[kp][:].bitcast(BF16)
                for kc in range(4):
                    TR(pb[:, kc * 128:(kc + 1) * 128], ybuf[:, par, ii, kc * 128:(kc + 1) * 128], [b_ybuf[par][ii]], [b_PS[kp]], inc=(kc == 3))
                CP("act", ytc[:, k2, :, ii * 128:(ii + 1) * 128], pb[:, 0:512].rearrange("p (c t) -> p c t", c=4),
                   [b_PS[kp]], [b_ytc[k2]])
            P.dma("sp", ysT_d[n_branch, :, :, c * 512:(c + 1) * 512], ytc[:, k2], reads=[b_ytc[k2]], writes=[b_ys[n_branch][c]])

        pre = {}

        def layer(l):
            vaug3 = vaug.rearrange("p (b k) -> p b k", b=16)
            qIT = aux[:, 0:4096].rearrange("p (c t) -> p c t", c=2)
            kIT = aux[:, 4096:6144]
            Mtm = aux[:, 6144:8192]
            mb = b_aux[12:16]
            rb3 = lambda ii: rbuf[:, ii, :].rearrange("p (h e) -> p h e", h=8)

            def q_dest(ch):
                return lambda tc: (qT[:, ch, tc * 512:(tc + 1) * 512], b_qT[ch][tc])

            def k_dest(ch):
                return lambda tc: (kT[:, ch, tc * 512:(tc + 1) * 512], b_kT[ch][tc])

            aux2v = aux2.rearrange("p j t -> p (j t)").rearrange("p (c t) -> p c t", c=4)

            def kz(kc8):
                if kc8 < 4:
                    return kT[:, kc8, :], b_kT[kc8]
                return aux2v[:, kc8 - 4, :], b_aux2[4 * (kc8 - 4):4 * (kc8 - 4) + 4]

            def kz_dest(ch):
                def f(tc):
                    (ta_, ba_), (tb_, bb_) = kz(2 * ch), kz(2 * ch + 1)
                    sl_ = slice(tc * 512, (tc + 1) * 512)
                    return (ta_[:, sl_], tb_[:, sl_]), (ba_[tc], bb_[tc])
                return f

            def kz_zero():
                for kc8 in range(8):
                    t_, b_ = kz(kc8)
                    ps_ = slice(64, 128) if kc8 % 2 == 0 else slice(0, 64)
                    P.op("pool", lambda e, t_=t_, ps_=ps_: e.memset(t_[ps_, :], 0.0), reads=[], writes=list(b_))

            qsrc_full = lambda h: (qT[:, h // 2, :], b_qT[h // 2])
            ksrc_z = lambda h: kz(h)

            def qsrc_std(h):
                p0 = (h % 2) * 64
                return qT[p0:p0 + 64, h // 2, :], b_qT[h // 2]

            unitsI = [unit_rope(l, f"qI{ch}", f"qIs{ch}", (lambda ch: lambda tc: (qIT[:, ch, tc * 512:(tc + 1) * 512], b_aux[ch * 4 + tc]))(ch))
                      for ch in range(2)] + \
                     [unit_rope(l, "kI0", "kIs0", lambda tc: (kIT[:, tc * 512:(tc + 1) * 512], b_aux[8 + tc]))]
            tposF = rbuf[:, 0, :]
            b_tpos = b_rbuf[0]
            unitsC = [unit_rope(l, f"qC{ch}", f"qCs{ch}", q_dest(ch), decay=(ch, +1, tposF, b_tpos)) for ch in range(4)] + \
                     [unit_rope(l, f"kC{ch}", f"kCs{ch}", kz_dest(ch), decay=(ch, -1, tposF, b_tpos)) for ch in range(4)]
            unitsB = [unit_rope(l, f"qB{ch}", f"qBs{ch}", q_dest(ch)) for ch in range(4)] + \
                     [unit_rope(l, "kBe0", "kBes0", k_dest(0)), unit_rope(l, "kBo0", "kBos0", k_dest(1))]
            unitsA = [unit_plain(l, f"qA{ch}", q_dest(ch)) for ch in range(4)] + [unit_plain(l, f"kA{ch}", kz_dest(ch)) for ch in range(4)]
            unitsD = [unit_rope(l, f"qD{ch}", f"qDs{ch}", q_dest(ch)) for ch in range(4)] + \
                     [unit_rope(l, f"kD{ch}", f"kDs{ch}", k_dest(ch)) for ch in range(4)]

            run_units(unitsI, pre.pop("I", None))
            nv = load_tm(l, "wI")
            for blk in range(NB):
                k = proj_tm_mm(blk, nv)
                P.op("act", lambda e, k=k, blk=blk: e.activation(out=wIt[:, blk, :], in_=PSt[k][:, 0:4], func=AF.Copy,
                                                                 scale=0.0625), [b_PS[k]], [b_wI])

            def idx_stream():
                order = [4 * c_ + ii_ for c_ in (3, 2, 1, 0) for ii_ in range(4)]

                def scores(i):
                    c = i // 4
                    nk = (i + 1) * 128
                    nch = (nk + 511) // 512
                    for kc5 in range(nch):
                        n = min(512, nk - kc5 * 512)
                        for g in range(4):
                            p0 = (g % 2) * 64
                            MM(PSt[3][:, 0:n], qIT[p0:p0 + 64, g // 2, i * 128:(i + 1) * 128], kIT[p0:p0 + 64, kc5 * 512:kc5 * 512 + n],
                               True, True, [b_aux[(g // 2) * 4 + c], b_aux[8 + kc5]], [b_PS[3]])
                            t1 = nxt("itmp", 2)
                            ACT(itmp[t1][:, 0:n], PSt[3][:, 0:n], AF.Relu, [b_PS[3]], [b_itmp[t1]])
                            a_sl = acc[:, kc5 * 512:kc5 * 512 + n]
                            if g == 0:
                                TS("dve", a_sl, itmp[t1][:, 0:n], wIt[:, i, 0:1], None, ALU.mult, None, [b_itmp[t1], b_wI], [b_acc[kc5]])
                            else:
                                STT(a_sl, itmp[t1][:, 0:n], wIt[:, i, g:g + 1], a_sl, ALU.mult, ALU.add, [b_itmp[t1], b_wI, b_acc[kc5]], [b_acc[kc5]])
                            yield 0.2 + n / 900.0
                    ab = b_acc[0:nch]
                    dg = acc[:, i * 128:(i + 1) * 128]
                    P.op("pool", lambda e, dg=dg: e.affine_select(out=dg, in_=dg, pattern=[[-1, 128]], compare_op=ALU.is_ge, fill=NEG,
                                                                  base=0, channel_multiplier=1), reads=ab, writes=ab)
                    yield 0.1

                def select(i):
                    nk = (i + 1) * 128
                    ab = b_acc[0:(nk + 511) // 512]
                    if i >= 2:
                        for r in range(32):
                            P.op("dve", lambda e, nk=nk: e.max(out=m8[:], in_=acc[:, 0:nk]), ab, [b_m8])
                            P.op("dve", lambda e, nk=nk: e.match_replace(out=acc[:, 0:nk], in_to_replace=m8[:], in_values=acc[:, 0:nk],
                                                                        imm_value=NEG), ab + [b_m8], ab)
                            yield 2.0 * (0.1 + nk / 960.0)
                        TS("dve", Mtm[:, 0:nk], acc[:, 0:nk], 0.5 * NEG, -30000.0, ALU.is_gt, ALU.mult, ab, mb)
                    else:
                        TS("dve", Mtm[:, 0:nk], acc[:, 0:nk], 0.5 * NEG, -30000.0, ALU.is_le, ALU.mult, ab, mb)
                    yield 0.1 + nk / 1500.0

                def transposes(i):
                    c, ii = i // 4, i % 4
                    q_ = i % 2
                    mts = xb[q_][:].bitcast(BF16).rearrange("p (j t) -> p j t", j=16)
                    for j in range(i + 1):
                        pb = PSt[3][:].bitcast(BF16)
                        TR(pb[:, 0:128], Mtm[:, j * 128:(j + 1) * 128], mb, [b_PS[3]])
                        mj = mts[:, j, :]
                        if j == i and i >= 2:
                            TT("dve", mj, pb[:, 0:128], Mc[:], ALU.mult, [b_PS[3], b_mask], [b_xb[q_]])
                            TT("dve", mj, mj, NegMp[:], ALU.add, [b_xb[q_], b_mask], [b_xb[q_]])
                        else:
                            CP("act", mj, pb[:, 0:128], [b_PS[3]], [b_xb[q_]])
                        yield 0.15
                    P.dma("sp", MT_d[c, :, 0:i + 1, ii * 128:(ii + 1) * 128], mts[:, 0:i + 1, :], reads=[b_xb[q_]], writes=[b_mtd[c]])
                    if ii == 3:
                        P.bg_mark += 1
                    yield 0.1

                yield from scores(order[0])
                for n_, i in enumerate(order):
                    yield from select(i)
                    if n_ + 1 < len(order):
                        yield from scores(order[n_ + 1])
                    yield from transposes(i)

            P.start_bg(idx_stream(), bg_ratio[0])
            fg0 = P.ninst_fg

            def fin_copy(with_den):
                def f(c, h, i, ko):
                    ii = i - 4 * c
                    CP("act", rbuf[:, ii, h * 64:(h + 1) * 64], PSt[ko][:, 0:64], [b_PS[ko]], [b_rbuf[ii]])
                    if with_den:
                        CP("act", dens[:, ii, h:h + 1], PSt[ko][:, 64:65], [b_PS[ko]], [b_rbuf[ii]])
                return f

            def done_std(nb, sink_l=None):
                def f(c):
                    par = c % 2
                    for ii in range(4):
                        dn_ = dens[:, ii, :]
                        if sink_l is not None:
                            TT("pool", dn_, dn_, esink[:, sink_l * 8:(sink_l + 1) * 8], ALU.add, [b_rbuf[ii], b_par], [b_rbuf[ii]])
                        TT("pool", dn_, dn_, negc[:, 0:8], ALU.pow, [b_rbuf[ii], b_negc], [b_rbuf[ii]])
                        TT("pool", ybuf[:, par, ii, :].rearrange("p (h e) -> p h e", h=8), rb3(ii),
                           dn_.unsqueeze(2).to_broadcast([128, 8, 64]), ALU.mult, [b_rbuf[ii]], [b_ybuf[par][ii]])
                    return lambda: y_transpose_store(nb, c, par)
                return f

            P.dma("sp", retn[:], retn_d[l].partition_broadcast(128), writes=[b_retn])
            P.op("pool", lambda e: e.iota(tposF, pattern=[[1, 512]], base=0, channel_multiplier=0, allow_small_or_imprecise_dtypes=True),
                 writes=[b_rbuf[0]])
            kz_zero()
            run_units(unitsC, pre.pop("C", None))
            nv = load_tm(l, "vC")
            for blk in range(NB):
                k = proj_tm_mm(blk, nv)
                CP("act", vaug3[:, blk, 0:512], PSt[k][:], [b_PS[k]], [b_v[blk]])
            load_tm(l, "gC")
            def done_C(c):
                par = c % 2
                for ii in range(4):
                    i = 4 * c + ii
                    r3 = rb3(ii)
                    st_, bst = smcol(32)
                    s1, s2, mean, rstd = st_[:, 0:8], st_[:, 8:16], st_[:, 16:24], st_[:, 24:32]
                    t1 = nxt("tmp", 4)
                    P.op("dve", lambda e, s1=s1, r3=r3: e.tensor_reduce(out=s1, in_=r3, axis=AX.X, op=ALU.add), [b_rbuf[ii]], bst)
                    TT("pool", tmp[t1][:], rbuf[:, ii, :], rbuf[:, ii, :], ALU.mult, [b_rbuf[ii]], [b_tmp[t1]])
                    sq3 = tmp[t1][:].rearrange("p (h e) -> p h e", h=8)
                    P.op("dve", lambda e, s2=s2, sq3=sq3: e.tensor_reduce(out=s2, in_=sq3, axis=AX.X, op=ALU.add), [b_tmp[t1]], bst)
                    TS("pool", mean, s1, 1.0 / 64.0, 0.0, ALU.mult, ALU.add, bst, bst)
                    TT("pool", s1, mean, mean, ALU.mult, bst, bst)
                    TS("pool", s2, s2, 1.0 / 64.0, EPS, ALU.mult, ALU.add, bst, bst)
                    TT("pool", s2, s2, s1, ALU.subtract, bst, bst)
                    TT("pool", rstd, s2, negc[:, 8:16], ALU.pow, bst + [b_negc], bst)
                    t3 = tmp[t1][:].rearrange("p (h e) -> p h e", h=8)
                    TT("pool", t3, r3, mean.unsqueeze(2).to_broadcast([128, 8, 64]), ALU.subtract, [b_rbuf[ii]] + bst, [b_tmp[t1]])
                    TT("pool", t3, t3, rstd.unsqueeze(2).to_broadcast([128, 8, 64]), ALU.mult, [b_tmp[t1]] + bst, [b_tmp[t1]])
                    TT("pool", tmp[t1][:], tmp[t1][:], retn[:], ALU.mult, [b_tmp[t1], b_retn], [b_tmp[t1]])
                    kg = proj_tm_mm(i, 512)
                    t2 = nxt("tmp", 4)
                    ACT(tmp[t2][:], PSt[kg][:], AF.Silu, [b_PS[kg]], [b_tmp[t2]])
                    TT("pool", ybuf[:, par, ii, :], tmp[t1][:], tmp[t2][:], ALU.mult, [b_tmp[t1], b_tmp[t2]], [b_ybuf[par][ii]])
                return lambda: y_transpose_store(2, c, par)

            hA = unitsA[0][0]()
            attn("C", l, 2, qsrc_full, ksrc_z,
                 lambda h, j: (vaug3[:, j, h * 64:(h + 1) * 64], b_v[j]), 64, fin_copy(False), None, done_C)

            kz_zero()
            run_units(unitsA, hA)
            vaug4 = vaug.rearrange("p (b h k) -> p b h k", b=16, h=8)
            nv = load_tm(l, "vA")
            for blk in range(NB):
                k = proj_tm_mm(blk, nv)
                CP("act", vaug4[:, blk, :, 0:64], PSt[k][:].rearrange("p (h k) -> p h k", h=8), [b_PS[k]], [b_v[blk]])
            P.op("pool", lambda e: e.memset(vaug4[:, :, :, 64:65], 1.0), reads=[], writes=b_v)
            nv = load_tm(l, "fA")
            for blk in range(NB):
                k = proj_tm_mm(blk, nv)
                P.op("act", lambda e, k=k, blk=blk: e.activation(out=lt[:, blk, :], in_=PSt[k][:, 0:8], func=AF.Copy), [b_PS[k]], [b_lt])
            ltf = lt[:].rearrange("p b h -> p (b h)")
            TT("pool", lt[:], lt[:], fbias[:, l * 8:(l + 1) * 8].unsqueeze(1).to_broadcast([128, 16, 8]), ALU.add, [b_lt, b_par], [b_lt])
            ACT(ltf, ltf, AF.Exp, [b_lt], [b_lt], scale=-1.0)
            ACT(ltf, ltf, AF.Ln, [b_lt], [b_lt], bias=1.0)
            dump("lt", ltf, [b_lt])
            kp = nxt("mm", NMMv[0])
            for blk in range(NB):
                for j in range(blk + 1):
                    lhs = TriF if j == blk else OnesF
                    MM(PSt[kp][:, blk * 8:(blk + 1) * 8], lhs[:], lt[:, j, :], j == 0, j == blk, [b_lt, b_cf], [b_PS[kp]])
            P.op("act", lambda e, kp=kp: e.activation(out=Ft[:].rearrange("p b h -> p (b h)"), in_=PSt[kp][:, 0:128], func=AF.Copy, scale=-1.0),
                 [b_PS[kp]], [b_F])
            kp = nxt("mm", NMMv[0])
            for c in range(4):
                MM(PSt[kp][:, c * 8:(c + 1) * 8], SelL[:], Ft[:, 4 * c + 3, :], True, True, [b_F, b_cf], [b_PS[kp]])
            CP("act", Fend[:].rearrange("p c h -> p (c h)"), PSt[kp][:, 0:32], [b_PS[kp]], [b_Fend])
            for c in range(4):
                TT("pool", biasA[:, c], Fend[:, c, :].unsqueeze(1).to_broadcast([128, 16, 8]), Ft[:], ALU.subtract,
                   [b_Fend, b_F], [b_biasA])
            dump("Ft", Ft[:].rearrange("p b h -> p (b h)"), [b_F])

            hD = unitsD[0][0]()
            attn("A", l, 0, qsrc_full, ksrc_z,
                 lambda h, j: (vaug4[:, j, h, :], b_v[j]), 65, fin_copy(True), None, done_std(0))

            run_units(unitsD, hD)
            vaugD = vaug[:, 0:16 * 130].rearrange("p (b g k) -> p b g k", b=16, g=2)
            nv = load_tm(l, "vD")
            for blk in range(NB):
                k = proj_tm_mm(blk, nv)
                CP("act", vaugD[:, blk, :, 0:64], PSt[k][:, 0:128].rearrange("p (g k) -> p g k", g=2), [b_PS[k]], [b_v[blk]])
            P.op("pool", lambda e: e.memset(vaugD[:, :, :, 64:65], 1.0), reads=[], writes=b_v)
            hB = unitsB[0][0]()
            attn("D", l, 3, qsrc_full, lambda h: (kT[:, (h // 4) * 2 + (h % 2), :], b_kT[(h // 4) * 2 + (h % 2)]),
                 lambda h, j: (vaugD[:, j, h // 4, :], b_v[j]), 65, fin_copy(True), None, done_std(3, sink_l=l))

            run_units(unitsB, hB)
            nv = load_tm(l, "vB")
            for blk in range(NB):
                k = proj_tm_mm(blk, nv)
                CP("act", vaug3[:, blk, 0:64], PSt[k][:, 0:64], [b_PS[k]], [b_v[blk]])
            P.op("pool", lambda e: e.memset(vaug3[:, :, 64:65], 1.0), reads=[], writes=b_v)

            def prep_B(c):
                if c == 0:
                    bg_stats[l] = (P.ninst_fg - fg0, P.bg_cost)
                P.drain_bg(until_mark=4 - c)
                if c == 0:
                    bg_stats[l] = (bg_stats[l][0], P.bg_cost)
                nj = 4 * c + 4
                P.dma("sp", aux2[:, 0:nj, :], MT_d[c, :, 0:nj, :], reads=[b_mtd[c]], writes=b_aux2[0:nj])

            attn("B", l, 1, qsrc_full, lambda h: (kT[:, h % 2, :], b_kT[h % 2]),
                 lambda h, j: (vaug3[:, j, 0:65], b_v[j]), 65, fin_copy(True), prep_B, done_std(1), corder=(3, 2, 1, 0))
            P.drain_bg()

            fence(RA_P2, RA_P3)
            fence(RB_P2, RB_P3)
            for n in range(4):
                for tc in range(4):
                    P.dma("sp", yT[:, n, :, tc * 512:(tc + 1) * 512], ysT_d[n, :, :, tc * 512:(tc + 1) * 512],
                          reads=[b_ys[n][tc]], writes=[b_yT[n][tc]])
            dump(f"yT_{l}", yT, RA_P3)
            P.dma("pool", woutt, wout[l], writes=[b_wout], stream="w")
            wbrt = wtmt[:].rearrange("p k (a c) -> p (k a) c", a=4)

            def load_wbr(dc):
                hf_ = dc % 2
                P.dma("pool", wbrt[:, 16 * hf_:16 * hf_ + 16, :].rearrange("p (n k) c -> p n k c", n=4),
                      wbr[l, :, dc].rearrange("n p k c -> p n k c"), writes=[b_wtmh[hf_]], stream="w")

            def unit_gate(dc, n):
                def ld():
                    return load_fm(l, f"g{n * 8 + dc}")

                def comp(hd):
                    wt, bw = hd
                    hf_ = dc % 2
                    if n == 0 and dc + 1 < 8:
                        load_wbr(dc + 1)
                    for tc in range(4):
                        kg = proj_fm_mm(wt, bw, tc)
                        kb_ = nxt("mm", NMMv[0])
                        for kc in range(4):
                            MM(PSt[kb_][:], wbrt[:, 16 * hf_ + n * 4 + kc, :], yT[:, n, kc, tc * 512:(tc + 1) * 512], kc == 0, kc == 3,
                               [b_wtmh[hf_], b_yT[n][tc]], [b_PS[kb_]])
                        t1 = nxt("tmp", 4)
                        ACT(tmp[t1][:], PSt[kg][:], AF.Sigmoid, [b_PS[kg]], [b_tmp[t1]])
                        ua = ubuf_p3[tc]
                        if n == 0:
                            TT("dve", ua, tmp[t1][:], PSt[kb_][:], ALU.mult, [b_tmp[t1], b_PS[kb_]], [b_macc[tc]])
                        else:
                            TT("dve", tmp[t1][:], tmp[t1][:], PSt[kb_][:], ALU.mult, [b_tmp[t1], b_PS[kb_]], [b_tmp[t1]])
                            if n < 3:
                                TT("pool", ua, ua, tmp[t1][:], ALU.add, [b_macc[tc], b_tmp[t1]], [b_macc[tc]])
                            else:
                                TT("pool", mergedT[:, dc, tc * 512:(tc + 1) * 512], ua, tmp[t1][:], ALU.add, [b_macc[tc], b_tmp[t1]],
                                   [b_mg[dc][tc]])
                return ld, comp

            load_wbr(0)
            unitsG = [unit_gate(dc, n) for dc in range(8) for n in range(4)]
            run_units(unitsG, None, depth=3)
            dump(f"mergedT_{l}", mergedT, [b for l_ in b_mg for b in l_])
            pre["F"] = (load_fm_up(l, 0), load_fm_up(l, 22))
            for blk in range(NB):
                halves = []
                for hf in range(2):
                    ko = 4 + nxt("acc", 4)
                    for kc in range(8):
                        MM(PSt[ko][:], mergedT[:, kc, blk * 128:(blk + 1) * 128], woutt[:, kc, hf * 512:(hf + 1) * 512], kc == 0, kc == 7,
                           [b_mg[kc][blk // 4], b_wout], [b_PS[ko]])
                    halves.append(ko)
                    if blk == 1 and hf == 0 and "ps1" in dbg_d:
                        t9 = nxt("tmp", 4)
                        CP("dve", tmp[t9][:], PSt[ko][:], [b_PS[ko]], [b_tmp[t9]])
                        dump("ps1", tmp[t9][:], [b_tmp[t9]])
                resid_norm(l, blk, halves, None, gff)
            dump(f"hT2_{l}", hT[:], b_hT)
            dump(f"xmid_{l}", xs_d, b_xs)

            fence(RA_P3, RA_P5)
            fence(RB_P3, RB_P5)
            P.dma("pool", wdnt, wdn[l], writes=[b_wdn], stream="w")
            if l == nl - 1:
                P.dma("sp", gfin_t, gfin_d.partition_broadcast(128), writes=b_wtmh)
            cv = convp[:, l * 176:(l + 1) * 176].rearrange("p (c k) -> p c k", c=44)
            hal = lt[:].rearrange("p b h -> p (b h)")[:, 0:88].rearrange("p (c k) -> p c k", c=44)
            last = (l == nl - 1)

            def down_proj(th):
                for bl in range(8):
                    blk = th * 8 + bl
                    halves = []
                    for hf in range(2):
                        ko = 4 + nxt("acc", 4)
                        for kc in range(22):
                            MM(PSt[ko][:], actT[:, kc, bl * 128:(bl + 1) * 128], wdnt[:, kc, hf * 512:(hf + 1) * 512], kc == 0, kc == 21,
                               [b_act[kc][bl // 4], b_wdn], [b_PS[ko]])
                        halves.append(ko)
                    resid_norm(l + 1 if not last else 0, blk, halves, None, gat, final=last)

            pend2 = []

            def stage2():
                while pend2:
                    cc_, tcl_, ta, tb = pend2.pop(0)
                    ACT(tmp[ta][:], tmp[ta][:], AF.Silu, [b_tmp[ta]], [b_tmp[ta]])
                    TT("pool", actT[:, cc_, tcl_ * 512:(tcl_ + 1) * 512], tmp[ta][:], tmp[tb][:], ALU.mult,
                       [b_tmp[ta], b_tmp[tb]], [b_act[cc_][tcl_]])

            def unit_ffn(th, cc):
                def ld():
                    return load_fm_up(l, cc), load_fm_up(l, 22 + cc)

                def comp(hd):
                    (wa, bwa), (wb_, bwb) = hd
                    for tcl in range(2):
                        tc = th * 2 + tcl
                        res = []
                        for which, (wt, bw, chn) in enumerate(((wa, bwa, cc), (wb_, bwb, 22 + cc))):
                            kp = proj_fm_mm(wt, bw, tc)
                            ku = which * 2 + (tcl % 2)
                            CP("act", ubuf[:, ku, 2:514], PSt[kp][:], [b_PS[kp]], [b_ub[ku]])
                        for which, chn in enumerate((cc, 22 + cc)):
                            ku = which * 2 + (tcl % 2)
                            u, bu, buh = ubuf[:, ku, :], b_ub[ku], b_ubh[ku]
                            w0, w1, w2, bb = (cv[:, chn, q_:q_ + 1] for q_ in range(4))
                            t1 = nxt("tmp", 4)
                            ACT(tmp[t1][:], u[:, 2:514], AF.Identity, [bu, b_par], [b_tmp[t1]], scale=w2, bias=bb)
                            if tc == 0:
                                P.op("dve", lambda e, u=u: e.memset(u[:, 0:2], 0.0), reads=[], writes=[buh])
                            elif tcl == 1:
                                kprev = which * 2
                                CP("dve", u[:, 0:2], ubuf[:, kprev, 512:514], [b_ub[kprev]], [buh])
                            else:
                                CP("dve", u[:, 0:2], hal[:, chn, :], [b_lt], [buh])
                            if tc == 1:
                                CP("dve", hal[:, chn, :], u[:, 512:514], [bu], [b_lt])
                            STT(tmp[t1][:], u[:, 1:513], w1, tmp[t1][:], ALU.mult, ALU.add, [bu, buh, b_par, b_tmp[t1]], [b_tmp[t1]])
                            STT(tmp[t1][:], u[:, 0:512], w0, tmp[t1][:], ALU.mult, ALU.add, [bu, buh, b_par, b_tmp[t1]], [b_tmp[t1]])
                            res.append(t1)
                        stage2()
                        pend2.append((cc, tcl, res[0], res[1]))
                    if cc == 21:
                        stage2()
                        down_proj(th)
                return ld, comp

            unitsF = [unit_ffn(th, cc) for th in range(2) for cc in range(22)]
            run_units(unitsF, pre.pop("F", None), depth=1)
            fence(RA_P5, RA_P2)
            fence(RB_P5, RB_P2)
            NMMv[0] = 3

        rr["acc"] = 0
        gfin_t = wtmt[:].rearrange("p k c -> p (k c)")[:, 0:2048].bitcast(F32)
        ubuf_p3 = [xb[0][:, 0:512], xb[0][:, 512:1024], xb[1][:, 0:512], xb[1][:, 512:1024]]
        b_macc = [b_xb[0], b_xb[0], b_xb[1], b_xb[1]]

        def load_fm_up(l, ch):
            k = nxt("wst", 4)
            P.dma("pool", wst[k][:], wup[l, ch], writes=[b_wst[k]], stream="w")
            return wst[k], b_wst[k]

        for blk in range(NB):
            resid_norm(0, blk, None, None, gat, first=True)
        dump("hT0", hT[:], b_hT)
        for l in range(nl):
            layer(l)
        P.wait_all("sp", b_out + [b_dbg])
        P.finish()
    return nc


def prep_weights(attn_norm, w_in, forget_bias, ret_norm, attn_sinks, w_branch, w_out,
                 ffn_norm, w_up, conv_w, conv_b, w_down, final_norm):
    f = lambda a: np.ascontiguousarray(np.asarray(a, dtype=np.float32))
    w_in = f(w_in)
    w_in = np.concatenate([w_in, np.zeros((w_in.shape[0], w_in.shape[1], 1), np.float32)], axis=2)
    fm_cols = np.stack([c for _, c in FM])
    wfm = w_in[:, :, fm_cols]
    wfm = wfm.reshape(L, 8, 128, NFM, 128).transpose(0, 3, 2, 1, 4)
    tm_cols = np.concatenate([c for _, c in TM_GROUPS])
    wtm = w_in[:, :, tm_cols].reshape(L, 8, 128, NTM).transpose(0, 2, 1, 3)
    wbr = f(w_branch).reshape(L, 4, 4, 128, 8, 128).transpose(0, 1, 4, 3, 2, 5)
    wout = f(w_out).reshape(L, 8, 128, 1024).transpose(0, 2, 1, 3)
    wup = f(w_up).reshape(L, 8, 128, 44, 128).transpose(0, 3, 2, 1, 4)
    wdn = f(w_down).reshape(L, 22, 128, 1024).transpose(0, 2, 1, 3)
    gattn = f(attn_norm).reshape(L, 8, 128).transpose(2, 0, 1).reshape(128, L * 8)
    gffn = f(ffn_norm).reshape(L, 8, 128).transpose(2, 0, 1).reshape(128, L * 8)
    cw = f(conv_w).reshape(L, 3, 44, 128)
    cb = f(conv_b).reshape(L, 1, 44, 128)
    convp = np.concatenate([cw, cb], axis=1).transpose(3, 0, 2, 1).reshape(128, L * 44 * 4)
    return {
        "wfm": f(wfm), "wtm": f(wtm), "wbr": f(wbr), "wout": f(wout), "wup": f(wup), "wdn": f(wdn),
        "gattn": f(gattn), "gffn": f(gffn), "gfin": f(final_norm).reshape(1, 1024),
        "fbias": f(forget_bias).reshape(1, L * 8), "retn": f(ret_norm).reshape(L, 1, 512),
        "sinks": f(attn_sinks).reshape(1, L * 8), "convp": f(convp),
    }


_NC_CACHE = {}


def kernel(x, attn_norm, w_in, forget_bias, ret_norm, attn_sinks, w_branch, w_out,
           ffn_norm, w_up, conv_w, conv_b, w_down, final_norm):
    x = np.asarray(x, dtype=np.float32)
    shared = prep_weights(attn_norm, w_in, forget_bias, ret_norm, attn_sinks, w_branch, w_out,
                          ffn_norm, w_up, conv_w, conv_b, w_down, final_norm)
    if "nc" not in _NC_CACHE:
        _NC_CACHE["nc"] = build_program()
    nc = _NC_CACHE["nc"]
    n = x.shape[0]
    in_maps = [dict(shared, x=np.ascontiguousarray(x[b])) for b in range(n)]
    res = run_bass_kernel_spmd(nc, in_maps, core_ids=list(range(n)))
    return np.stack([np.asarray(r["out"], dtype=np.float32) for r in res.results], axis=0)
```

```python
import math
import os
from contextlib import ExitStack

import numpy as np
import concourse.bass as bass
import concourse.mybir as mybir
from concourse.bass_utils import run_bass_kernel_spmd

F32 = mybir.dt.float32
BF16 = mybir.dt.bfloat16
I32 = mybir.dt.int32
ALU = mybir.AluOpType
AF = mybir.ActivationFunctionType
AX = mybir.AxisListType

S = 2048
D = 1024
L = 2
NB = 16
DFF = 2816
EPS = 1e-6
NEG = -1.0e30
SEM_LIMIT = 24000


class Buf:
    __slots__ = ("name", "w", "r")

    def __init__(self, name=""):
        self.name = name
        self.w = None
        self.r = []


def bufs(n, name=""):
    return [Buf(f"{name}{i}") for i in range(n)]


class Prog:
    ENGS = ("pe", "dve", "act", "pool", "sp")

    def __init__(self, nc, stack, nsems=40):
        self.nc = nc
        self.q = {e: [] for e in self.ENGS}
        self.sempool = [stack.enter_context(nc.semaphore(f"s{i}")) for i in range(nsems)]
        self.nsem = 0
        self.cur = {}
        self.seen = {e: {} for e in self.ENGS}
        self.simq = {e: [] for e in self.ENGS}
        self.ninst = 0
        self.ninst_fg = 0
        self.bg = None
        self.bg_ratio = 0.0
        self.bg_steps = 0
        self._bg_acc = 0.0
        self._in_bg = False

    def start_bg(self, gen, ratio):
        self.bg, self.bg_ratio, self._bg_acc, self.bg_steps = gen, ratio, 0.0, 0
        self.bg_cost = 0.0
        self.bg_mark = 0

    def _bg_step(self):
        self._in_bg = True
        w = 0.0
        try:
            w = next(self.bg) or 0.0
            self.bg_steps += 1
            self.bg_cost += w
        except StopIteration:
            self.bg = None
        finally:
            self._in_bg = False
        return w

    def _tick(self):
        if self._in_bg:
            return
        self.ninst_fg += 1
        if self.bg is None:
            return
        self._bg_acc += self.bg_ratio
        while self._bg_acc > 0.0 and self.bg is not None:
            self._bg_acc -= max(self._bg_step(), 1e-3)

    def drain_bg(self, until_mark=None):
        while self.bg is not None and (until_mark is None or self.bg_mark < until_mark):
            self._bg_step()

    def _stream(self, name):
        if name not in self.cur:
            self.cur[name] = [self._newsem(), 0]
        return self.cur[name]

    def _newsem(self):
        i = self.nsem
        self.nsem += 1
        assert i < len(self.sempool), "out of semaphores"
        return i

    def _bump(self, name, by):
        st = self._stream(name)
        st[1] += by
        tok = (st[0], st[1])
        if st[1] >= SEM_LIMIT:
            self.cur[name] = [self._newsem(), 0]
        return tok

    def _peek(self, name, by):
        st = self._stream(name)
        return (st[0], st[1] + by)

    def _deps(self, eng, reads, writes, is_dma):
        need = {}

        def add(tok, kind):
            if tok is None:
                return
            e2, si, v = tok
            if (not is_dma) and e2 == eng and eng != "pool" and kind != "raw":
                return
            if need.get(si, 0) < v:
                need[si] = v
        for b in reads:
            add(b.w, "raw")
        for b in writes:
            add(b.w, "waw")
            for t in b.r:
                add(t, "war")
        out = []
        seen = self.seen[eng]
        for si, v in need.items():
            if seen.get(si, 0) < v:
                seen[si] = v
                out.append((si, v))
        return out

    def _commit(self, tok3, reads, writes):
        for b in reads:
            b.r.append(tok3)
            if len(b.r) > 48:
                mx = {}
                for (e, si, v) in b.r:
                    if si not in mx or mx[si][2] < v:
                        mx[si] = (e, si, v)
                b.r = list(mx.values())
        for b in writes:
            b.w = tok3
            b.r = []

    def op(self, eng, fn, reads=(), writes=(), inc=True):
        waits = self._deps(eng, reads, writes, False)
        if inc:
            si, v = self._bump(eng, 1)
        else:
            si, v = self._peek(eng, 1)
        sem = self.sempool[si]
        sems = self.sempool

        def emit(e, fn=fn, waits=waits, inc=inc, sem=sem):
            for (wi, wv) in waits:
                e.wait_ge(sems[wi], wv)
            ins = fn(e)
            if inc:
                ins.then_inc(sem, 1)
        self.q[eng].append(emit)
        self.simq[eng].append((tuple(waits), (si, 1) if inc else None, len(self.simq[eng])))
        self._commit((eng, si, v), reads, writes)
        self.ninst += 1
        self._tick()

    DMA_SLOTS = {"x": 10, "w": 8}

    def dma(self, eng, out, in_, reads=(), writes=(), stream="x", **kw):
        if not hasattr(self, "dstreams"):
            self.dstreams = {}
        if stream not in self.dstreams:
            k = self.DMA_SLOTS.get(stream, 4)
            self.dstreams[stream] = {"sems": [[self._newsem(), 0] for _ in range(k)], "next": 0}
        ds = self.dstreams[stream]
        slot = ds["next"]
        ds["next"] = (slot + 1) % len(ds["sems"])
        ent = ds["sems"][slot]
        if ent[1] + 16 > SEM_LIMIT:
            prev = tuple(ent)
            ent[0], ent[1] = self._newsem(), 0
        else:
            prev = tuple(ent)
        waits = self._deps(eng, reads, writes, True)
        if prev[1] > 0 and self.seen[eng].get(prev[0], 0) < prev[1]:
            self.seen[eng][prev[0]] = prev[1]
            waits.append(prev)
        ent[1] += 16
        si, v = ent[0], ent[1]
        sem = self.sempool[si]
        sems = self.sempool

        def emit(e, waits=waits, sem=sem, out=out, in_=in_, kw=kw):
            for (wi, wv) in waits:
                e.wait_ge(sems[wi], wv)
            e.dma_start(out=out, in_=in_, **kw).then_inc(sem, 16)
        self.q[eng].append(emit)
        self.simq[eng].append((tuple(waits), (si, 16), len(self.simq[eng])))
        self._commit(("dma", si, v), reads, writes)
        self.ninst += 1
        self._tick()

    def wait_all(self, eng, bl):
        waits = self._deps(eng, bl, (), True)
        sems = self.sempool

        def emit(e, waits=waits):
            for (wi, wv) in waits:
                e.wait_ge(sems[wi], wv)
        self.q[eng].append(emit)

    def check_deadlock(self):
        sem = {}
        pos = {e: 0 for e in self.ENGS}
        progress = True
        while progress:
            progress = False
            for e in self.ENGS:
                q = self.simq[e]
                while pos[e] < len(q):
                    waits, inc, _ = q[pos[e]]
                    if any(sem.get(si, 0) < v for (si, v) in waits):
                        break
                    if inc is not None:
                        sem[inc[0]] = sem.get(inc[0], 0) + inc[1]
                    pos[e] += 1
                    progress = True
        stuck = {e: (pos[e], len(self.simq[e])) for e in self.ENGS if pos[e] < len(self.simq[e])}
        if stuck:
            msg = []
            for e, (p, n) in stuck.items():
                waits, inc, _ = self.simq[e][p]
                msg.append(f"{e}: stuck at {p}/{n} waits={[(si, v, sem.get(si, 0)) for (si, v) in waits if sem.get(si, 0) < v]}")
            raise RuntimeError("DEADLOCK in emitted program: " + " | ".join(msg))

    def finish(self):
        self.check_deadlock()
        nc = self.nc
        q = self.q
        with nc.Block() as block:
            @block.tensor
            def _(e):
                for f in q["pe"]:
                    f(e)

            @block.vector
            def _(e):
                for f in q["dve"]:
                    f(e)

            @block.scalar
            def _(e):
                for f in q["act"]:
                    f(e)

            @block.gpsimd
            def _(e):
                for f in q["pool"]:
                    f(e)

            @block.sync
            def _(e):
                for f in q["sp"]:
                    f(e)


COL_SIZES = (512, 512, 512, 8, 512, 64, 64, 256, 64, 4, 512, 512, 512, 512, 512, 128, 128, 4096)
COL_NAMES = ("qA", "kA", "vA", "fA", "qB", "kB", "vB", "qI", "kI", "wI", "qC", "kC", "vC", "gC",
             "qD", "kD", "vD", "gates")
COL_OFF = {}
_o = 0
for _n, _s in zip(COL_NAMES, COL_SIZES):
    COL_OFF[_n] = (_o, _s)
    _o += _s
N_IN = _o


def _cols(name):
    o, s = COL_OFF[name]
    return np.arange(o, o + s)


def _swap(cols):
    c = cols.reshape(-1, 64)
    return np.concatenate([c[:, 32:], c[:, :32]], axis=1).reshape(-1)


def _dup(cols64):
    return np.concatenate([cols64, cols64])


ZCOL = sum(COL_SIZES)


def _padlo(cols64):
    return np.concatenate([cols64, np.full(64, ZCOL)])


def _padhi(cols64):
    return np.concatenate([np.full(64, ZCOL), cols64])


def fm_chunk_cols():
    ch = []

    def add(name, cols):
        cols = np.asarray(cols)
        assert cols.size % 128 == 0
        for i in range(cols.size // 128):
            ch.append((f"{name}{i}", cols[i * 128:(i + 1) * 128]))
    add("qC", _cols("qC")); add("qCs", _swap(_cols("qC")))
    add("kC", _cols("kC")); add("kCs", _swap(_cols("kC")))
    add("qB", _cols("qB")); add("qBs", _swap(_cols("qB")))
    add("kBe", _padlo(_cols("kB"))); add("kBes", _padlo(_swap(_cols("kB"))))
    add("kBo", _padhi(_cols("kB"))); add("kBos", _padhi(_swap(_cols("kB"))))
    add("qI", _cols("qI")); add("qIs", _swap(_cols("qI")))
    add("kI", _dup(_cols("kI"))); add("kIs", _dup(_swap(_cols("kI"))))
    add("qA", _cols("qA")); add("kA", _cols("kA"))
    add("qD", _cols("qD")); add("qDs", _swap(_cols("qD")))
    kd = _cols("kD")
    kds = _swap(kd)
    add("kD", np.concatenate([_padlo(kd[:64]), _padhi(kd[:64]), _padlo(kd[64:]), _padhi(kd[64:])]))
    add("kDs", np.concatenate([_padlo(kds[:64]), _padhi(kds[:64]), _padlo(kds[64:]), _padhi(kds[64:])]))
    add("g", _cols("gates"))
    return ch


FM = fm_chunk_cols()
FM_IDX = {n: i for i, (n, _) in enumerate(FM)}
NFM = len(FM)

TM_GROUPS = [("vC", _cols("vC")), ("gC", _cols("gC")),
             ("vB", _cols("vB")), ("wI", _cols("wI")),
             ("vA", _cols("vA")), ("fA", _cols("fA")), ("vD", _cols("vD"))]
TM_OFF = {}
_o = 0
for _n, _c in TM_GROUPS:
    TM_OFF[_n] = (_o, _c.size)
    _o += _c.size
NTM = _o

LOG_G = [math.log(1.0 - 2.0 ** (-5.0 - h)) for h in range(8)]


def build_program(nl=L, dbg=()):
    stats = {}
    _build(nl, dbg, [0.2], stats)
    fg, steps = stats.get(0, (1, 0))
    ratio = steps / max(1.0, 0.95 * fg)
    if os.environ.get("KDEBUG"):
        print("bg calibration: fg ops", fg, "bg steps", steps, "ratio", ratio)
    return _build(nl, dbg, [ratio], {})


def _build(nl, dbg, bg_ratio, bg_stats):
    nc = bass.Bass("TRN2", target_bir_lowering=False)
    dram = lambda name, shape, dt, kind: nc.dram_tensor(name, list(shape), dt, kind=kind).ap()
    x_in = dram("x", [S, D], F32, "ExternalInput")
    wfm = dram("wfm", [L, NFM, 128, 8, 128], F32, "ExternalInput")
    wtm = dram("wtm", [L, 128, 8, NTM], F32, "ExternalInput")
    wbr = dram("wbr", [L, 4, 8, 128, 4, 128], F32, "ExternalInput")
    wout = dram("wout", [L, 128, 8, 1024], F32, "ExternalInput")
    wup = dram("wup", [L, 44, 128, 8, 128], F32, "ExternalInput")
    wdn = dram("wdn", [L, 128, 22, 1024], F32, "ExternalInput")
    gat_d = dram("gattn", [128, L * 8], F32, "ExternalInput")
    gff_d = dram("gffn", [128, L * 8], F32, "ExternalInput")
    gfin_d = dram("gfin", [1, 1024], F32, "ExternalInput")
    fbias_d = dram("fbias", [1, L * 8], F32, "ExternalInput")
    retn_d = dram("retn", [L, 1, 512], F32, "ExternalInput")
    sinks_d = dram("sinks", [1, L * 8], F32, "ExternalInput")
    convp_d = dram("convp", [128, L * 44 * 4], F32, "ExternalInput")
    out_d = dram("out", [S, D], F32, "ExternalOutput")
    xs_d = dram("xs", [S, D], F32, "Internal")
    ysT_d = dram("ysT", [4, 128, 4, S], BF16, "Internal")
    MT_d = dram("MTd", [4, 128, 16, 512], BF16, "Internal")
    dbg_d = {}
    for name, shape, dt in dbg:
        dbg_d[name] = dram("dbg_" + name, shape, dt, "ExternalOutput")

    with ExitStack() as st:
        P = Prog(nc, st)
        sbt = lambda name, shape, dt: st.enter_context(nc.sbuf_tensor(name, list(shape), dt))
        pst = lambda name, shape, dt: st.enter_context(nc.psum_tensor(name, list(shape), dt))

        ident = sbt("ident", [128, 128], BF16); b_ident = Buf()
        Mc = sbt("Mc", [128, 128], BF16); Mp = sbt("Mp", [128, 128], BF16); b_mask = Buf()
        NegMp = sbt("NegMp", [128, 128], BF16)
        TriF = sbt("TriF", [128, 128], F32); OnesF = sbt("OnesF", [128, 128], F32); SelL = sbt("SelL", [128, 128], F32)
        b_cf = Buf()
        ropeC = sbt("ropeC", [128, S], BF16); ropeS = sbt("ropeS", [128, S], BF16); b_rope = Buf()
        gat = sbt("gat", [128, L * 8], F32); gff = sbt("gff", [128, L * 8], F32)
        fbias = sbt("fbiasb", [128, L * 8], F32); esink = sbt("esink", [128, L * 8], F32)
        convp = sbt("convpb", [128, L * 44 * 4], F32)
        b_par = Buf()
        retn = sbt("retnb", [128, 512], F32); b_retn = Buf()
        small = sbt("small", [128, 64], F32)
        b_small = bufs(64, "sm")
        hT = sbt("hT", [128, 8, S], BF16); b_hT = bufs(NB, "hT")
        RA = sbt("RA", [128, 32896], BF16)
        RB = sbt("RB", [128, 24576], BF16)
        wst = [sbt(f"wst{i}", [128, 8, 128], BF16) for i in range(4)]; b_wst = bufs(4, "wst")
        wtmt = sbt("wtmt", [128, 8, 512], BF16); b_wtmh = bufs(2, "wtm")
        PT = [sbt(f"PT{i}", [128, 512], BF16) for i in range(3)]; b_PT = bufs(3, "PT")
        ET = [sbt(f"ET{i}", [128, 512], BF16) for i in range(2)]; b_ET = bufs(2, "ET")
        tmp = [sbt(f"tmp{i}", [128, 512], F32) for i in range(4)]; b_tmp = bufs(4, "tmp")
        xb = [sbt(f"xb{i}", [128, 1024], F32) for i in range(2)]; b_xb = bufs(2, "xb")
        xn = sbt("xn", [128, 1024], BF16); b_xn = Buf()
        Ft = sbt("Ft", [128, 16, 8], F32); b_F = Buf()
        lt = sbt("lt", [128, 16, 8], F32); b_lt = Buf()
        Fend = sbt("Fend", [128, 4, 8], F32); b_Fend = Buf()
        biasA = sbt("biasA", [128, 4, 16, 8], F32); b_biasA = Buf()
        wIt = sbt("wIt", [128, 16, 4], F32); b_wI = Buf()
        m8 = sbt("m8", [128, 8], F32); b_m8 = Buf()
        dummy = sbt("dummyt", [128, 2], F32); b_dummy = Buf()
        itmp = [sbt(f"itmp{i}", [128, 512], F32) for i in range(2)]; b_itmp = bufs(2, "itmp")
        dens = sbt("dens", [128, 4, 8], F32)
        negc = sbt("negc", [128, 16], F32); b_negc = Buf()
        b_mtd = bufs(4, "mtd")
        lgt = sbt("lgt", [128, 40], F32); b_lgt = Buf()
        NMMv = [3]
        PSt = [pst(f"ps{i}", [128, 512], F32) for i in range(8)]; b_PS = bufs(8, "ps")

        qT = RA[:, 0:8192].rearrange("p (c t) -> p c t", c=4)
        kT = RA[:, 8192:16384].rearrange("p (c t) -> p c t", c=4)
        vaug = RA[:, 16384:16384 + 8320]
        aux = RA[:, 24704:24704 + 8192]
        b_qT = [bufs(4) for _ in range(4)]
        b_kT = [bufs(4) for _ in range(4)]
        b_v = bufs(NB, "v")
        b_aux = bufs(16, "aux")
        RA_P2 = [b for l_ in b_qT for b in l_] + [b for l_ in b_kT for b in l_] + b_v + b_aux
        yT = RA[:, 0:32768].rearrange("p (n c t) -> p n c t", n=4, c=4)
        b_yT = [bufs(4) for _ in range(4)]
        RA_P3 = [b for l_ in b_yT for b in l_]
        actT = RA[:, 0:22528].rearrange("p (c t) -> p c t", c=22)
        b_act = [bufs(2) for _ in range(22)]
        ubuf = RA[:, 22528:22528 + 4 * 1032].bitcast(F32).rearrange("p (k t) -> p k t", k=4)
        b_ub = bufs(4, "ub")
        b_ubh = bufs(4, "ubh")
        RA_P5 = [b for l_ in b_act for b in l_] + b_ub + b_ubh
        aux2 = RB[:, 0:8192].rearrange("p (j t) -> p j t", j=16)
        b_aux2 = bufs(16, "aux2")
        acc = RB[:, 8192:12288].bitcast(F32)
        b_acc = bufs(4, "acc")
        rbuf = RB[:, 12288:16384].bitcast(F32).rearrange("p (i t) -> p i t", i=4)
        b_rbuf = bufs(4, "rbuf")
        ybuf = RB[:, 16384:20480].rearrange("p (k i t) -> p k i t", k=2, i=4)
        b_ybuf = [bufs(4) for _ in range(2)]
        ytc = RB[:, 20480:24576].rearrange("p (k c t) -> p k c t", k=2, c=4)
        b_ytc = bufs(2, "ytc")
        RB_P2 = b_aux2 + b_acc + b_rbuf + [b for l_ in b_ybuf for b in l_] + b_ytc
        mergedT = RB[:, 0:16384].rearrange("p (c t) -> p c t", c=8)
        b_mg = [bufs(4) for _ in range(8)]
        woutt = RB[:, 16384:24576].rearrange("p (c n) -> p c n", c=8)
        b_wout = Buf()
        RB_P3 = [b for l_ in b_mg for b in l_] + [b_wout]
        wdnt = RB[:, 0:22528].rearrange("p (c n) -> p c n", c=22)
        b_wdn = Buf()
        RB_P5 = [b_wdn]
        isc = [RB[:, k * 4096:(k + 1) * 4096].bitcast(F32) for k in range(3)]
        RB_INIT = bufs(3, "isc")

        b_xs = bufs(NB, "xs")
        b_ys = [bufs(4) for _ in range(4)]
        b_out = bufs(NB, "out")
        b_dbg = Buf()

        def ACT(out, in_, func, reads, writes, **kw):
            P.op("act", lambda e: e.activation(out=out, in_=in_, func=func, **kw), reads, writes)

        def TT(eng, out, in0, in1, op, reads, writes):
            P.op(eng, lambda e: e.tensor_tensor(out=out, in0=in0, in1=in1, op=op), reads, writes)

        def TS(eng, out, in0, s1, s2, op0, op1, reads, writes, **kw):
            if op1 is None:
                P.op(eng, lambda e: e.tensor_scalar(out=out, in0=in0, scalar1=s1, scalar2=None, op0=op0, **kw), reads, writes)
            else:
                P.op(eng, lambda e: e.tensor_scalar(out=out, in0=in0, scalar1=s1, scalar2=s2, op0=op0, op1=op1, **kw), reads, writes)

        def STT(out, in0, scalar, in1, op0, op1, reads, writes):
            P.op("dve", lambda e: e.scalar_tensor_tensor(out=out, in0=in0, scalar=scalar, in1=in1, op0=op0, op1=op1), reads, writes)

        def CP(eng, out, in_, reads, writes):
            if eng == "act":
                P.op("act", lambda e: e.activation(out=out, in_=in_, func=AF.Copy), reads, writes)
            else:
                P.op(eng, lambda e: e.tensor_copy(out=out, in_=in_), reads, writes)

        def MM(out, lhsT, rhs, start, stop, reads, writes, inc=None):
            if inc is None:
                inc = bool(stop)
            P.op("pe", lambda e: e.matmul(out, lhsT=lhsT, rhs=rhs, start=start, stop=stop), reads, writes, inc=inc)

        def TR(out, in_, reads, writes, inc=True):
            P.op("pe", lambda e: e.transpose(out, in_, ident[:]), list(reads) + [b_ident], writes, inc=inc)

        def fence(old, new):
            P.op("pool", lambda e: e.memset(dummy[:, 0:1], 0.0), reads=[], writes=list(old) + list(new) + [b_dummy])

        rr = {"mm": 0, "tmp": 0, "PT": 0, "ET": 0, "wst": 0, "xb": 0, "sm": 0, "itmp": 0, "acc": 0}

        def nxt(kind, n):
            v = rr[kind]
            rr[kind] = (v + 1) % n
            return v

        def smcol(n=1):
            v = rr["sm"]
            if v + n > 64:
                v = 0
            rr["sm"] = (v + n) % 64
            return small[:, v:v + n], b_small[v:v + n]

        def dump(name, src_ap, reads):
            if name in dbg_d:
                P.dma("sp", dbg_d[name], src_ap, reads=reads, writes=[b_dbg])

        P.op("pool", lambda e: e.memset(dummy[:], 0.0), writes=[b_dummy])
        P.op("pool", lambda e: e.memset(negc[:, 0:8], -1.0), writes=[b_negc])
        P.op("pool", lambda e: e.memset(negc[:, 8:16], -0.5), writes=[b_negc])
        for ch_ in range(4):
            for half_ in range(2):
                lg_ = LOG_G[2 * ch_ + half_]
                ps_ = slice(64 * half_, 64 * half_ + 64)
                P.op("pool", lambda e, ps_=ps_, ch_=ch_, lg_=lg_: e.memset(lgt[ps_, ch_:ch_ + 1], lg_), writes=[b_lgt])
                P.op("pool", lambda e, ps_=ps_, ch_=ch_, lg_=lg_: e.memset(lgt[ps_, 4 + ch_:5 + ch_], -lg_), writes=[b_lgt])
                for tc_ in range(4):
                    P.op("pool", lambda e, ps_=ps_, ch_=ch_, lg_=lg_, tc_=tc_: e.memset(lgt[ps_, 8 + ch_ * 4 + tc_:9 + ch_ * 4 + tc_], lg_ * 512.0 * tc_), writes=[b_lgt])
                    P.op("pool", lambda e, ps_=ps_, ch_=ch_, lg_=lg_, tc_=tc_: e.memset(lgt[ps_, 24 + ch_ * 4 + tc_:25 + ch_ * 4 + tc_], -lg_ * 512.0 * tc_), writes=[b_lgt])
        i0 = isc[0][:, 0:128]
        P.op("pool", lambda e: e.memset(i0, 0.0), writes=[RB_INIT[0]])
        P.op("pool", lambda e: e.affine_select(out=i0, in_=i0, pattern=[[-1, 128]], compare_op=ALU.not_equal, fill=1.0,
                                               base=0, channel_multiplier=1), reads=[RB_INIT[0]], writes=[RB_INIT[0]])
        CP("dve", ident[:], i0, [RB_INIT[0]], [b_ident])
        i1 = isc[0][:, 128:256]
        P.op("pool", lambda e: e.memset(i1, 1.0), writes=[RB_INIT[0]])
        P.op("pool", lambda e: e.affine_select(out=TriF[:], in_=i1, pattern=[[1, 128]], compare_op=ALU.is_ge, fill=0.0,
                                               base=0, channel_multiplier=-1), reads=[RB_INIT[0]], writes=[b_cf])
        CP("dve", Mc[:], TriF[:], [b_cf], [b_mask])
        TS("dve", Mp[:], TriF[:], -1.0, 1.0, ALU.mult, ALU.add, [b_cf], [b_mask])
        TS("dve", NegMp[:], TriF[:], 30000.0, -30000.0, ALU.mult, ALU.add, [b_cf], [b_mask])
        P.op("pool", lambda e: e.memset(OnesF[:], 1.0), writes=[b_cf])
        i2 = isc[0][:, 256:384]
        P.op("pool", lambda e: e.memset(i2, 0.0), writes=[RB_INIT[0]])
        P.op("pool", lambda e: e.affine_select(out=SelL[:], in_=i2, pattern=[[0, 128]], compare_op=ALU.not_equal, fill=1.0,
                                               base=-127, channel_multiplier=1), reads=[RB_INIT[0]], writes=[b_cf])
        P.dma("sp", gat[:], gat_d, writes=[b_par])
        P.dma("sp", gff[:], gff_d, writes=[b_par])
        P.dma("sp", fbias[:], fbias_d.partition_broadcast(128), writes=[b_par])
        P.dma("sp", esink[:], sinks_d.partition_broadcast(128), writes=[b_par])
        P.dma("sp", convp[:], convp_d, writes=[b_par])
        ACT(esink[:], esink[:], AF.Exp, [b_par], [b_par])
        bs0 = b_small[0:12]
        pf = small[:, 0:1]
        P.op("pool", lambda e: e.iota(pf, pattern=[[0, 1]], base=0, channel_multiplier=1, allow_small_or_imprecise_dtypes=True), writes=bs0)
        ki = small[:, 1:2].bitcast(I32)
        TS("dve", small[:, 2:3], pf, 1.0 / 64.0, 0.5 / 64.0 - 0.5, ALU.mult, ALU.add, bs0, bs0)
        CP("dve", ki, small[:, 2:3], bs0, bs0)
        CP("dve", small[:, 2:3], ki, bs0, bs0)
        STT(small[:, 3:4], small[:, 2:3], -64.0, pf, ALU.mult, ALU.add, bs0, bs0)
        TS("dve", small[:, 4:5], small[:, 3:4], 1.0 / 32.0, 0.5 / 32.0 - 0.5, ALU.mult, ALU.add, bs0, bs0)
        CP("dve", ki, small[:, 4:5], bs0, bs0)
        CP("dve", small[:, 4:5], ki, bs0, bs0)
        STT(small[:, 3:4], small[:, 4:5], -32.0, small[:, 3:4], ALU.mult, ALU.add, bs0, bs0)
        ACT(small[:, 5:6], small[:, 3:4], AF.Exp, bs0, bs0, scale=-2.0 * math.log(10000.0) / 64.0)
        TS("dve", small[:, 5:6], small[:, 5:6], 1.0 / (2.0 * math.pi), None, ALU.mult, None, bs0, bs0)
        TS("dve", small[:, 6:7], small[:, 4:5], 2.0, -1.0, ALU.mult, ALU.add, bs0, bs0)
        upos, kint, kf = isc[0], isc[1].bitcast(I32), isc[2]
        P.op("pool", lambda e: e.iota(upos, pattern=[[1, S]], base=0, channel_multiplier=0, allow_small_or_imprecise_dtypes=True),
             reads=[RB_INIT[0]], writes=[RB_INIT[0]])
        TS("dve", upos, upos, small[:, 5:6], None, ALU.mult, None, [RB_INIT[0]] + bs0, [RB_INIT[0]])
        SIN_SCALE = 6.2831845
        for which in range(2):
            if which == 1:
                TS("dve", upos, upos, 0.25, None, ALU.add, None, [RB_INIT[0]], [RB_INIT[0]])
            CP("dve", kint, upos, [RB_INIT[0]], [RB_INIT[1]])
            CP("dve", kf, kint, [RB_INIT[1]], [RB_INIT[2]])
            TT("dve", kf, upos, kf, ALU.subtract, [RB_INIT[0], RB_INIT[2]], [RB_INIT[2]])
            ACT(kf, kf, AF.Sin, [RB_INIT[2]], [RB_INIT[2]], scale=SIN_SCALE)
            if which == 0:
                TS("dve", ropeS[:], kf, small[:, 6:7], None, ALU.mult, None, [RB_INIT[2]] + bs0, [b_rope])
            else:
                CP("dve", ropeC[:], kf, [RB_INIT[2]], [b_rope])
        dump("ropeC", ropeC[:], [b_rope]); dump("ropeS", ropeS[:], [b_rope])
        fence(RB_INIT + bs0, RB_P2 + b_small)

        def load_fm(l, name):
            k = nxt("wst", 4)
            P.dma("pool", wst[k][:], wfm[l, FM_IDX[name]], writes=[b_wst[k]], stream="w")
            return wst[k], b_wst[k]

        def proj_fm_mm(wt, bw, tc):
            k = nxt("mm", NMMv[0])
            hb = b_hT[tc * 4:(tc + 1) * 4]
            for kc in range(8):
                MM(PSt[k][:], wt[:, kc, :], hT[:, kc, tc * 512:(tc + 1) * 512], kc == 0, kc == 7, [bw] + hb, [b_PS[k]])
            return k

        def unit_plain(l, name, dest_fn):
            def ld():
                return load_fm(l, name)

            def comp(hd):
                wt, bw = hd
                for tc in range(4):
                    k = proj_fm_mm(wt, bw, tc)
                    dst, dbuf = dest_fn(tc)
                    if isinstance(dst, tuple):
                        CP("act", dst[0][0:64], PSt[k][0:64, :], [b_PS[k]], [dbuf[0]])
                        CP("act", dst[1][64:128], PSt[k][64:128, :], [b_PS[k]], [dbuf[1]])
                    else:
                        CP("act", dst, PSt[k][:], [b_PS[k]], [dbuf])
            return ld, comp

        def unit_rope(l, name, sname, dest_fn, decay=None):
            def ld():
                return load_fm(l, name), load_fm(l, sname)

            def comp(hd):
                (wt, bw), (ws, bws) = hd
                for tc in range(4):
                    k1 = proj_fm_mm(wt, bw, tc)
                    k2 = proj_fm_mm(ws, bws, tc)
                    dst, dbuf = dest_fn(tc)
                    t1 = nxt("tmp", 4); t2 = nxt("tmp", 4)
                    sl = slice(tc * 512, (tc + 1) * 512)
                    CP("act", tmp[t1][:], PSt[k1][:], [b_PS[k1]], [b_tmp[t1]])
                    CP("act", tmp[t2][:], PSt[k2][:], [b_PS[k2]], [b_tmp[t2]])
                    rr["rope"] = (rr.get("rope", 0) + 1) % 3
                    en_ = "dve" if rr["rope"] == 0 else "pool"
                    TT(en_, tmp[t1][:], tmp[t1][:], ropeC[:, sl], ALU.mult, [b_tmp[t1], b_rope], [b_tmp[t1]])
                    TT(en_, tmp[t2][:], tmp[t2][:], ropeS[:, sl], ALU.mult, [b_tmp[t2], b_rope], [b_tmp[t2]])
                    if decay is None:
                        TT(en_, dst, tmp[t1][:], tmp[t2][:], ALU.add, [b_tmp[t1], b_tmp[t2]], [dbuf])
                    else:
                        ch_, sgn_, tposF, b_tpos = decay
                        o_ = 0 if sgn_ > 0 else 4
                        ob_ = 8 if sgn_ > 0 else 24
                        ke = nxt("ET", 2)
                        P.op("act", lambda e, ke=ke, ch_=ch_, o_=o_, ob_=ob_, tc=tc: e.activation(
                            out=ET[ke][:], in_=tposF, func=AF.Exp, scale=lgt[:, o_ + ch_:o_ + ch_ + 1],
                            bias=lgt[:, ob_ + ch_ * 4 + tc:ob_ + ch_ * 4 + tc + 1]), [b_tpos, b_lgt], [b_ET[ke]])
                        TT(en_, tmp[t1][:], tmp[t1][:], tmp[t2][:], ALU.add, [b_tmp[t1], b_tmp[t2]], [b_tmp[t1]])
                        if isinstance(dst, tuple):
                            TT(en_, dst[0][0:64], tmp[t1][0:64, :], ET[ke][0:64, :], ALU.mult, [b_tmp[t1], b_ET[ke]], [dbuf[0]])
                            TT(en_, dst[1][64:128], tmp[t1][64:128, :], ET[ke][64:128, :], ALU.mult, [b_tmp[t1], b_ET[ke]], [dbuf[1]])
                        else:
                            TT(en_, dst, tmp[t1][:], ET[ke][:], ALU.mult, [b_tmp[t1], b_ET[ke]], [dbuf])
            return ld, comp

        def run_units(units, first=None, depth=1):
            q_ = [first if first is not None else units[0][0]()]
            nxt_ld = 1
            for k, (ld, comp) in enumerate(units):
                while nxt_ld < len(units) and nxt_ld <= k + depth:
                    q_.append(units[nxt_ld][0]())
                    nxt_ld += 1
                comp(q_.pop(0))

        def load_tm(l, gname):
            o, n = TM_OFF[gname]
            P.dma("pool", wtmt[:, :, 0:n], wtm[l, :, :, o:o + n], writes=b_wtmh, stream="w")
            return n

        def proj_tm_mm(blk, n):
            k = nxt("mm", NMMv[0])
            for kc in range(8):
                MM(PSt[k][:, 0:n], hT[:, kc, blk * 128:(blk + 1) * 128], wtmt[:, kc, 0:n], kc == 0, kc == 7,
                   b_wtmh + [b_hT[blk]], [b_PS[k]])
            return k

        def resid_norm(l_next, blk, ps_halves, src, gtile, first=False, final=False):
            k = nxt("xb", 2)
            xt, bx = xb[k], b_xb[k]
            if first:
                P.dma("sp", xt[:], x_in[blk * 128:(blk + 1) * 128, :], writes=[bx])
            else:
                P.dma("sp", xt[:], xs_d[blk * 128:(blk + 1) * 128, :], reads=[b_xs[blk]], writes=[bx])
            if ps_halves is not None:
                for hf, kp in enumerate(ps_halves):
                    sl = slice(hf * 512, (hf + 1) * 512)
                    TT("dve", xt[:, sl], xt[:, sl], PSt[kp][:], ALU.add, [bx, b_PS[kp]], [bx])
                if not final:
                    P.dma("sp", xs_d[blk * 128:(blk + 1) * 128, :], xt[:], reads=[bx], writes=[b_xs[blk]])
            elif first:
                P.dma("sp", xs_d[blk * 128:(blk + 1) * 128, :], xt[:], reads=[bx], writes=[b_xs[blk]])
            ss, bss = smcol(2)
            ACT(xn[:], xt[:], AF.Square, [bx], [b_xn] + bss, accum_out=ss[:, 0:1])
            ACT(ss[:, 1:2], ss[:, 0:1], AF.Sqrt, bss, bss, scale=1.0 / D, bias=EPS)
            P.op("dve", lambda e: e.reciprocal(out=ss[:, 1:2], in_=ss[:, 1:2]), bss, bss)
            if final:
                TS("dve", xt[:], xt[:], ss[:, 1:2], None, ALU.mult, None, [bx] + bss, [bx])
                TT("pool", xt[:], xt[:], gfin_t, ALU.mult, [bx] + b_wtmh, [bx])
                P.dma("sp", out_d[blk * 128:(blk + 1) * 128, :], xt[:], reads=[bx], writes=[b_out[blk]])
                return
            TS("dve", xn[:], xt[:], ss[:, 1:2], None, ALU.mult, None, [bx] + bss, [b_xn])
            kp = nxt("mm", NMMv[0])
            pb = PSt[kp][:].bitcast(BF16)
            for kc in range(8):
                TR(pb[:, kc * 128:(kc + 1) * 128], xn[:, kc * 128:(kc + 1) * 128], [b_xn], [b_PS[kp]], inc=(kc == 7))
            g3 = gtile[:, l_next * 8:(l_next + 1) * 8].unsqueeze(2).to_broadcast([128, 8, 128])
            TT("dve", hT[:, :, blk * 128:(blk + 1) * 128], pb.rearrange("p (c t) -> p c t", c=8), g3, ALU.mult,
               [b_PS[kp], b_par], [b_hT[blk]])

        def attn(kind, l, n_branch, qsrc, ksrc, vsrc, vcols, finalize, chunk_prep=None, chunk_done=None, LA=1, corder=(0, 1, 2, 3)):
            tiles = []
            for c in corder:
                for h in range(8):
                    if kind == "D":
                        jlist = list(range(max(4 * c - 1, 0), 4 * c + 4))
                    else:
                        jlist = list(range(0, 4 * c + 4))
                    for j in jlist:
                        tiles.append((c, h, j, h == 0 and j == jlist[0], h == 7 and j == jlist[-1]))

            def emit_S(c, h, j):
                qa, qb = qsrc(h)
                ka, kb = ksrc(h)
                i0 = max(j, 4 * c)
                i1 = min(j + 1, 4 * c + 3) if kind == "D" else 4 * c + 3
                n = (i1 - i0 + 1) * 128
                ks = nxt("mm", 3)
                if kind == "B":
                    MM(PSt[ks][:, 0:n], ka[:, j * 128:(j + 1) * 128], qa[:, i0 * 128:(i1 + 1) * 128], True, False,
                       [kb[j // 4], qb[c]], [b_PS[ks]])
                    offb = (i0 - 4 * c) * 128
                    MM(PSt[ks][:, 0:n], ident[:], aux2[:, j, offb:offb + n], False, True, [b_ident, b_aux2[j]], [b_PS[ks]])
                else:
                    MM(PSt[ks][:, 0:n], ka[:, j * 128:(j + 1) * 128], qa[:, i0 * 128:(i1 + 1) * 128], True, True,
                       [kb[j // 4], qb[c]], [b_PS[ks]])
                kp = nxt("PT", 3)
                pt, bpt = PT[kp], b_PT[kp]
                if kind == "A":
                    ACT(pt[:, 0:n], PSt[ks][:, 0:n], AF.Exp, [b_PS[ks], b_biasA], [bpt],
                        bias=biasA[:, c, j, h:h + 1], scale=0.125)
                    if j >= 4 * c:
                        TT("pool", pt[:, 0:128], pt[:, 0:128], Mc[:], ALU.mult, [bpt, b_mask], [bpt])
                elif kind == "B":
                    off = (i0 - 4 * c) * 128
                    ACT(pt[:, 0:n], PSt[ks][:, 0:n], AF.Exp, [b_PS[ks]], [bpt], scale=0.125)
                elif kind == "C":
                    P.op("act", lambda e, ks=ks, n=n, pt=pt: e.activation(out=pt[:, 0:n], in_=PSt[ks][:, 0:n], func=AF.Copy, scale=0.125),
                         [b_PS[ks]], [bpt])
                    if j >= 4 * c:
                        TT("pool", pt[:, 0:128], pt[:, 0:128], Mc[:], ALU.mult, [bpt, b_mask], [bpt])
                else:
                    ACT(pt[:, 0:n], PSt[ks][:, 0:n], AF.Exp, [b_PS[ks]], [bpt], scale=0.125)
                    for i in range(i0, i1 + 1):
                        o = (i - i0) * 128
                        mk = Mc if i == j else Mp
                        TT("pool", pt[:, o:o + 128], pt[:, o:o + 128], mk[:], ALU.mult, [bpt, b_mask], [bpt])
                return (c, h, j, i0, i1, pt, bpt)

            def emit_PV(info, last_of_chunk):
                c, h, j, i0, i1, pt, bpt = info
                for i in range(i0, i1 + 1):
                    va, vb = vsrc(h, j)
                    ko = 4 + (i - 4 * c)
                    first = (j == max(i - 1, 0)) if kind == "D" else (j == 0)
                    o = (i - i0) * 128
                    MM(PSt[ko][:, 0:vcols], pt[:, o:o + 128], va, first, j == i, [bpt, vb], [b_PS[ko]],
                       inc=(j == i or i == i1))
                    if j == i:
                        finalize(c, h, i, ko)
                if last_of_chunk and chunk_done is not None:
                    part2 = chunk_done(c)
                    if part2 is not None:
                        deferred.append([6, part2])

            def tick_deferred(flush=False):
                for d in list(deferred):
                    d[0] -= 1
                    if d[0] <= 0 or flush:
                        deferred.remove(d)
                        d[1]()

            deferred = []
            pend = []
            for (c, h, j, first_of_chunk, last_of_chunk) in tiles:
                if first_of_chunk and chunk_prep is not None:
                    chunk_prep(c)
                pend.append((emit_S(c, h, j), last_of_chunk))
                if len(pend) > LA:
                    emit_PV(*pend.pop(0))
                    tick_deferred()
            while pend:
                emit_PV(*pend.pop(0))
            tick_deferred(flush=True)

        def y_transpose_store(n_branch, c, par):
            k2 = c % 2
            for ii in range(4):
                kp = nxt("mm", NMMv[0])
                pb = PSt[kp][:].bitcast(BF16)
                for kc in range(4):
                    TR(pb[:, kc * 128:(kc + 1) * 128], ybuf[:, par, ii, kc * 128:(kc + 1) * 128], [b_ybuf[par][ii]], [b_PS[kp]], inc=(kc == 3))
                CP("act", ytc[:, k2, :, ii * 128:(ii + 1) * 128], pb[:, 0:512].rearrange("p (c t) -> p c t", c=4),
                   [b_PS[kp]], [b_ytc[k2]])
            P.dma("sp", ysT_d[n_branch, :, :, c * 512:(c + 1) * 512], ytc[:, k2], reads=[b_ytc[k2]], writes=[b_ys[n_branch][c]])

        pre = {}

        def layer(l):
            vaug3 = vaug.rearrange("p (b k) -> p b k", b=16)
            qIT = aux[:, 0:4096].rearrange("p (c t) -> p c t", c=2)
            kIT = aux[:, 4096:6144]
            Mtm = aux[:, 6144:8192]
            mb = b_aux[12:16]
            rb3 = lambda ii: rbuf[:, ii, :].rearrange("p (h e) -> p h e", h=8)

            def q_dest(ch):
                return lambda tc: (qT[:, ch, tc * 512:(tc + 1) * 512], b_qT[ch][tc])

            def k_dest(ch):
                return lambda tc: (kT[:, ch, tc * 512:(tc + 1) * 512], b_kT[ch][tc])

            aux2v = aux2.rearrange("p j t -> p (j t)").rearrange("p (c t) -> p c t", c=4)

            def kz(kc8):
                if kc8 < 4:
                    return kT[:, kc8, :], b_kT[kc8]
                return aux2v[:, kc8 - 4, :], b_aux2[4 * (kc8 - 4):4 * (kc8 - 4) + 4]

            def kz_dest(ch):
                def f(tc):
                    (ta_, ba_), (tb_, bb_) = kz(2 * ch), kz(2 * ch + 1)
                    sl_ = slice(tc * 512, (tc + 1) * 512)
                    return (ta_[:, sl_], tb_[:, sl_]), (ba_[tc], bb_[tc])
                return f

            def kz_zero():
                for kc8 in range(8):
                    t_, b_ = kz(kc8)
                    ps_ = slice(64, 128) if kc8 % 2 == 0 else slice(0, 64)
                    P.op("pool", lambda e, t_=t_, ps_=ps_: e.memset(t_[ps_, :], 0.0), reads=[], writes=list(b_))

            qsrc_full = lambda h: (qT[:, h // 2, :], b_qT[h // 2])
            ksrc_z = lambda h: kz(h)

            def qsrc_std(h):
                p0 = (h % 2) * 64
                return qT[p0:p0 + 64, h // 2, :], b_qT[h // 2]

            unitsI = [unit_rope(l, f"qI{ch}", f"qIs{ch}", (lambda ch: lambda tc: (qIT[:, ch, tc * 512:(tc + 1) * 512], b_aux[ch * 4 + tc]))(ch))
                      for ch in range(2)] + \
                     [unit_rope(l, "kI0", "kIs0", lambda tc: (kIT[:, tc * 512:(tc + 1) * 512], b_aux[8 + tc]))]
            tposF = rbuf[:, 0, :]
            b_tpos = b_rbuf[0]
            unitsC = [unit_rope(l, f"qC{ch}", f"qCs{ch}", q_dest(ch), decay=(ch, +1, tposF, b_tpos)) for ch in range(4)] + \
                     [unit_rope(l, f"kC{ch}", f"kCs{ch}", kz_dest(ch), decay=(ch, -1, tposF, b_tpos)) for ch in range(4)]
            unitsB = [unit_rope(l, f"qB{ch}", f"qBs{ch}", q_dest(ch)) for ch in range(4)] + \
                     [unit_rope(l, "kBe0", "kBes0", k_dest(0)), unit_rope(l, "kBo0", "kBos0", k_dest(1))]
            unitsA = [unit_plain(l, f"qA{ch}", q_dest(ch)) for ch in range(4)] + [unit_plain(l, f"kA{ch}", kz_dest(ch)) for ch in range(4)]
            unitsD = [unit_rope(l, f"qD{ch}", f"qDs{ch}", q_dest(ch)) for ch in range(4)] + \
                     [unit_rope(l, f"kD{ch}", f"kDs{ch}", k_dest(ch)) for ch in range(4)]

            run_units(unitsI, pre.pop("I", None))
            nv = load_tm(l, "wI")
            for blk in range(NB):
                k = proj_tm_mm(blk, nv)
                P.op("act", lambda e, k=k, blk=blk: e.activation(out=wIt[:, blk, :], in_=PSt[k][:, 0:4], func=AF.Copy,
                                                                 scale=0.0625), [b_PS[k]], [b_wI])

            def idx_stream():
                order = [4 * c_ + ii_ for c_ in (3, 2, 1, 0) for ii_ in range(4)]

                def scores(i):
                    c = i // 4
                    nk = (i + 1) * 128
                    nch = (nk + 511) // 512
                    for kc5 in range(nch):
                        n = min(512, nk - kc5 * 512)
                        for g in range(4):
                            p0 = (g % 2) * 64
                            MM(PSt[3][:, 0:n], qIT[p0:p0 + 64, g // 2, i * 128:(i + 1) * 128], kIT[p0:p0 + 64, kc5 * 512:kc5 * 512 + n],
                               True, True, [b_aux[(g // 2) * 4 + c], b_aux[8 + kc5]], [b_PS[3]])
                            t1 = nxt("itmp", 2)
                            ACT(itmp[t1][:, 0:n], PSt[3][:, 0:n], AF.Relu, [b_PS[3]], [b_itmp[t1]])
                            a_sl = acc[:, kc5 * 512:kc5 * 512 + n]
                            if g == 0:
                                TS("dve", a_sl, itmp[t1][:, 0:n], wIt[:, i, 0:1], None, ALU.mult, None, [b_itmp[t1], b_wI], [b_acc[kc5]])
                            else:
                                STT(a_sl, itmp[t1][:, 0:n], wIt[:, i, g:g + 1], a_sl, ALU.mult, ALU.add, [b_itmp[t1], b_wI, b_acc[kc5]], [b_acc[kc5]])
                            yield 0.2 + n / 900.0
                    ab = b_acc[0:nch]
                    dg = acc[:, i * 128:(i + 1) * 128]
                    P.op("pool", lambda e, dg=dg: e.affine_select(out=dg, in_=dg, pattern=[[-1, 128]], compare_op=ALU.is_ge, fill=NEG,
                                                                  base=0, channel_multiplier=1), reads=ab, writes=ab)
                    yield 0.1

                def select(i):
                    nk = (i + 1) * 128
                    ab = b_acc[0:(nk + 511) // 512]
                    if i >= 2:
                        for r in range(32):
                            P.op("dve", lambda e, nk=nk: e.max(out=m8[:], in_=acc[:, 0:nk]), ab, [b_m8])
                            P.op("dve", lambda e, nk=nk: e.match_replace(out=acc[:, 0:nk], in_to_replace=m8[:], in_values=acc[:, 0:nk],
                                                                        imm_value=NEG), ab + [b_m8], ab)
                            yield 2.0 * (0.1 + nk / 960.0)
                        TS("dve", Mtm[:, 0:nk], acc[:, 0:nk], 0.5 * NEG, -30000.0, ALU.is_gt, ALU.mult, ab, mb)
                    else:
                        TS("dve", Mtm[:, 0:nk], acc[:, 0:nk], 0.5 * NEG, -30000.0, ALU.is_le, ALU.mult, ab, mb)
                    yield 0.1 + nk / 1500.0

                def transposes(i):
                    c, ii = i // 4, i % 4
                    q_ = i % 2
                    mts = xb[q_][:].bitcast(BF16).rearrange("p (j t) -> p j t", j=16)
                    for j in range(i + 1):
                        pb = PSt[3][:].bitcast(BF16)
                        TR(pb[:, 0:128], Mtm[:, j * 128:(j + 1) * 128], mb, [b_PS[3]])
                        mj = mts[:, j, :]
                        if j == i and i >= 2:
                            TT("dve", mj, pb[:, 0:128], Mc[:], ALU.mult, [b_PS[3], b_mask], [b_xb[q_]])
                            TT("dve", mj, mj, NegMp[:], ALU.add, [b_xb[q_], b_mask], [b_xb[q_]])
                        else:
                            CP("act", mj, pb[:, 0:128], [b_PS[3]], [b_xb[q_]])
                        yield 0.15
                    P.dma("sp", MT_d[c, :, 0:i + 1, ii * 128:(ii + 1) * 128], mts[:, 0:i + 1, :], reads=[b_xb[q_]], writes=[b_mtd[c]])
                    if ii == 3:
                        P.bg_mark += 1
                    yield 0.1

                yield from scores(order[0])
                for n_, i in enumerate(order):
                    yield from select(i)
                    if n_ + 1 < len(order):
                        yield from scores(order[n_ + 1])
                    yield from transposes(i)

            P.start_bg(idx_stream(), bg_ratio[0])
            fg0 = P.ninst_fg

            def fin_copy(with_den):
                def f(c, h, i, ko):
                    ii = i - 4 * c
                    CP("act", rbuf[:, ii, h * 64:(h + 1) * 64], PSt[ko][:, 0:64], [b_PS[ko]], [b_rbuf[ii]])
                    if with_den:
                        CP("act", dens[:, ii, h:h + 1], PSt[ko][:, 64:65], [b_PS[ko]], [b_rbuf[ii]])
                return f

            def done_std(nb, sink_l=None):
                def f(c):
                    par = c % 2
                    for ii in range(4):
                        dn_ = dens[:, ii, :]
                        if sink_l is not None:
                            TT("pool", dn_, dn_, esink[:, sink_l * 8:(sink_l + 1) * 8], ALU.add, [b_rbuf[ii], b_par], [b_rbuf[ii]])
                        TT("pool", dn_, dn_, negc[:, 0:8], ALU.pow, [b_rbuf[ii], b_negc], [b_rbuf[ii]])
                        TT("pool", ybuf[:, par, ii, :].rearrange("p (h e) -> p h e", h=8), rb3(ii),
                           dn_.unsqueeze(2).to_broadcast([128, 8, 64]), ALU.mult, [b_rbuf[ii]], [b_ybuf[par][ii]])
                    return lambda: y_transpose_store(nb, c, par)
                return f

            P.dma("sp", retn[:], retn_d[l].partition_broadcast(128), writes=[b_retn])
            P.op("pool", lambda e: e.iota(tposF, pattern=[[1, 512]], base=0, channel_multiplier=0, allow_small_or_imprecise_dtypes=True),
                 writes=[b_rbuf[0]])
            kz_zero()
            run_units(unitsC, pre.pop("C", None))
            nv = load_tm(l, "vC")
            for blk in range(NB):
                k = proj_tm_mm(blk, nv)
                CP("act", vaug3[:, blk, 0:512], PSt[k][:], [b_PS[k]], [b_v[blk]])
            load_tm(l, "gC")
            def done_C(c):
                par = c % 2
                for ii in range(4):
                    i = 4 * c + ii
                    r3 = rb3(ii)
                    st_, bst = smcol(32)
                    s1, s2, mean, rstd = st_[:, 0:8], st_[:, 8:16], st_[:, 16:24], st_[:, 24:32]
                    t1 = nxt("tmp", 4)
                    P.op("dve", lambda e, s1=s1, r3=r3: e.tensor_reduce(out=s1, in_=r3, axis=AX.X, op=ALU.add), [b_rbuf[ii]], bst)
                    TT("pool", tmp[t1][:], rbuf[:, ii, :], rbuf[:, ii, :], ALU.mult, [b_rbuf[ii]], [b_tmp[t1]])
                    sq3 = tmp[t1][:].rearrange("p (h e) -> p h e", h=8)
                    P.op("dve", lambda e, s2=s2, sq3=sq3: e.tensor_reduce(out=s2, in_=sq3, axis=AX.X, op=ALU.add), [b_tmp[t1]], bst)
                    TS("pool", mean, s1, 1.0 / 64.0, 0.0, ALU.mult, ALU.add, bst, bst)
                    TT("pool", s1, mean, mean, ALU.mult, bst, bst)
                    TS("pool", s2, s2, 1.0 / 64.0, EPS, ALU.mult, ALU.add, bst, bst)
                    TT("pool", s2, s2, s1, ALU.subtract, bst, bst)
                    TT("pool", rstd, s2, negc[:, 8:16], ALU.pow, bst + [b_negc], bst)
                    t3 = tmp[t1][:].rearrange("p (h e) -> p h e", h=8)
                    TT("pool", t3, r3, mean.unsqueeze(2).to_broadcast([128, 8, 64]), ALU.subtract, [b_rbuf[ii]] + bst, [b_tmp[t1]])
                    TT("pool", t3, t3, rstd.unsqueeze(2).to_broadcast([128, 8, 64]), ALU.mult, [b_tmp[t1]] + bst, [b_tmp[t1]])
                    TT("pool", tmp[t1][:], tmp[t1][:], retn[:], ALU.mult, [b_tmp[t1], b_retn], [b_tmp[t1]])
                    kg = proj_tm_mm(i, 512)
                    t2 = nxt("tmp", 4)
                    ACT(tmp[t2][:], PSt[kg][:], AF.Silu, [b_PS[kg]], [b_tmp[t2]])
                    TT("pool", ybuf[:, par, ii, :], tmp[t1][:], tmp[t2][:], ALU.mult, [b_tmp[t1], b_tmp[t2]], [b_ybuf[par][ii]])
                return lambda: y_transpose_store(2, c, par)

            hA = unitsA[0][0]()
            attn("C", l, 2, qsrc_full, ksrc_z,
                 lambda h, j: (vaug3[:, j, h * 64:(h + 1) * 64], b_v[j]), 64, fin_copy(False), None, done_C)

            kz_zero()
            run_units(unitsA, hA)
            vaug4 = vaug.rearrange("p (b h k) -> p b h k", b=16, h=8)
            nv = load_tm(l, "vA")
            for blk in range(NB):
                k = proj_tm_mm(blk, nv)
                CP("act", vaug4[:, blk, :, 0:64], PSt[k][:].rearrange("p (h k) -> p h k", h=8), [b_PS[k]], [b_v[blk]])
            P.op("pool", lambda e: e.memset(vaug4[:, :, :, 64:65], 1.0), reads=[], writes=b_v)
            nv = load_tm(l, "fA")
            for blk in range(NB):
                k = proj_tm_mm(blk, nv)
                P.op("act", lambda e, k=k, blk=blk: e.activation(out=lt[:, blk, :], in_=PSt[k][:, 0:8], func=AF.Copy), [b_PS[k]], [b_lt])
            ltf = lt[:].rearrange("p b h -> p (b h)")
            TT("pool", lt[:], lt[:], fbias[:, l * 8:(l + 1) * 8].unsqueeze(1).to_broadcast([128, 16, 8]), ALU.add, [b_lt, b_par], [b_lt])
            ACT(ltf, ltf, AF.Exp, [b_lt], [b_lt], scale=-1.0)
            ACT(ltf, ltf, AF.Ln, [b_lt], [b_lt], bias=1.0)
            dump("lt", ltf, [b_lt])
            kp = nxt("mm", NMMv[0])
            for blk in range(NB):
                for j in range(blk + 1):
                    lhs = TriF if j == blk else OnesF
                    MM(PSt[kp][:, blk * 8:(blk + 1) * 8], lhs[:], lt[:, j, :], j == 0, j == blk, [b_lt, b_cf], [b_PS[kp]])
            P.op("act", lambda e, kp=kp: e.activation(out=Ft[:].rearrange("p b h -> p (b h)"), in_=PSt[kp][:, 0:128], func=AF.Copy, scale=-1.0),
                 [b_PS[kp]], [b_F])
            kp = nxt("mm", NMMv[0])
            for c in range(4):
                MM(PSt[kp][:, c * 8:(c + 1) * 8], SelL[:], Ft[:, 4 * c + 3, :], True, True, [b_F, b_cf], [b_PS[kp]])
            CP("act", Fend[:].rearrange("p c h -> p (c h)"), PSt[kp][:, 0:32], [b_PS[kp]], [b_Fend])
            for c in range(4):
                TT("pool", biasA[:, c], Fend[:, c, :].unsqueeze(1).to_broadcast([128, 16, 8]), Ft[:], ALU.subtract,
                   [b_Fend, b_F], [b_biasA])
            dump("Ft", Ft[:].rearrange("p b h -> p (b h)"), [b_F])

            hD = unitsD[0][0]()
            attn("A", l, 0, qsrc_full, ksrc_z,
                 lambda h, j: (vaug4[:, j, h, :], b_v[j]), 65, fin_copy(True), None, done_std(0))

            run_units(unitsD, hD)
            vaugD = vaug[:, 0:16 * 130].rearrange("p (b g k) -> p b g k", b=16, g=2)
            nv = load_tm(l, "vD")
            for blk in range(NB):
                k = proj_tm_mm(blk, nv)
                CP("act", vaugD[:, blk, :, 0:64], PSt[k][:, 0:128].rearrange("p (g k) -> p g k", g=2), [b_PS[k]], [b_v[blk]])
            P.op("pool", lambda e: e.memset(vaugD[:, :, :, 64:65], 1.0), reads=[], writes=b_v)
            hB = unitsB[0][0]()
            attn("D", l, 3, qsrc_full, lambda h: (kT[:, (h // 4) * 2 + (h % 2), :], b_kT[(h // 4) * 2 + (h % 2)]),
                 lambda h, j: (vaugD[:, j, h // 4, :], b_v[j]), 65, fin_copy(True), None, done_std(3, sink_l=l))

            run_units(unitsB, hB)
            nv = load_tm(l, "vB")
            for blk in range(NB):
                k = proj_tm_mm(blk, nv)
                CP("act", vaug3[:, blk, 0:64], PSt[k][:, 0:64], [b_PS[k]], [b_v[blk]])
            P.op("pool", lambda e: e.memset(vaug3[:, :, 64:65], 1.0), reads=[], writes=b_v)

            def prep_B(c):
                if c == 0:
                    bg_stats[l] = (P.ninst_fg - fg0, P.bg_cost)
                P.drain_bg(until_mark=4 - c)
                if c == 0:
                    bg_stats[l] = (bg_stats[l][0], P.bg_cost)
                nj = 4 * c + 4
                P.dma("sp", aux2[:, 0:nj, :], MT_d[c, :, 0:nj, :], reads=[b_mtd[c]], writes=b_aux2[0:nj])

            attn("B", l, 1, qsrc_full, lambda h: (kT[:, h % 2, :], b_kT[h % 2]),
                 lambda h, j: (vaug3[:, j, 0:65], b_v[j]), 65, fin_copy(True), prep_B, done_std(1), corder=(3, 2, 1, 0))
            P.drain_bg()

            fence(RA_P2, RA_P3)
            fence(RB_P2, RB_P3)
            for n in range(4):
                for tc in range(4):
                    P.dma("sp", yT[:, n, :, tc * 512:(tc + 1) * 512], ysT_d[n, :, :, tc * 512:(tc + 1) * 512],
                          reads=[b_ys[n][tc]], writes=[b_yT[n][tc]])
            dump(f"yT_{l}", yT, RA_P3)
            P.dma("pool", woutt, wout[l], writes=[b_wout], stream="w")
            wbrt = wtmt[:].rearrange("p k (a c) -> p (k a) c", a=4)

            def load_wbr(dc):
                hf_ = dc % 2
                P.dma("pool", wbrt[:, 16 * hf_:16 * hf_ + 16, :].rearrange("p (n k) c -> p n k c", n=4),
                      wbr[l, :, dc].rearrange("n p k c -> p n k c"), writes=[b_wtmh[hf_]], stream="w")

            def unit_gate(dc, n):
                def ld():
                    return load_fm(l, f"g{n * 8 + dc}")

                def comp(hd):
                    wt, bw = hd
                    hf_ = dc % 2
                    if n == 0 and dc + 1 < 8:
                        load_wbr(dc + 1)
                    for tc in range(4):
                        kg = proj_fm_mm(wt, bw, tc)
                        kb_ = nxt("mm", NMMv[0])
                        for kc in range(4):
                            MM(PSt[kb_][:], wbrt[:, 16 * hf_ + n * 4 + kc, :], yT[:, n, kc, tc * 512:(tc + 1) * 512], kc == 0, kc == 3,
                               [b_wtmh[hf_], b_yT[n][tc]], [b_PS[kb_]])
                        t1 = nxt("tmp", 4)
                        ACT(tmp[t1][:], PSt[kg][:], AF.Sigmoid, [b_PS[kg]], [b_tmp[t1]])
                        ua = ubuf_p3[tc]
                        if n == 0:
                            TT("dve", ua, tmp[t1][:], PSt[kb_][:], ALU.mult, [b_tmp[t1], b_PS[kb_]], [b_macc[tc]])
                        else:
                            TT("dve", tmp[t1][:], tmp[t1][:], PSt[kb_][:], ALU.mult, [b_tmp[t1], b_PS[kb_]], [b_tmp[t1]])
                            if n < 3:
                                TT("pool", ua, ua, tmp[t1][:], ALU.add, [b_macc[tc], b_tmp[t1]], [b_macc[tc]])
                            else:
                                TT("pool", mergedT[:, dc, tc * 512:(tc + 1) * 512], ua, tmp[t1][:], ALU.add, [b_macc[tc], b_tmp[t1]],
                                   [b_mg[dc][tc]])
                return ld, comp

            load_wbr(0)
            unitsG = [unit_gate(dc, n) for dc in range(8) for n in range(4)]
            run_units(unitsG, None, depth=3)
            dump(f"mergedT_{l}", mergedT, [b for l_ in b_mg for b in l_])
            pre["F"] = (load_fm_up(l, 0), load_fm_up(l, 22))
            for blk in range(NB):
                halves = []
                for hf in range(2):
                    ko = 4 + nxt("acc", 4)
                    for kc in range(8):
                        MM(PSt[ko][:], mergedT[:, kc, blk * 128:(blk + 1) * 128], woutt[:, kc, hf * 512:(hf + 1) * 512], kc == 0, kc == 7,
                           [b_mg[kc][blk // 4], b_wout], [b_PS[ko]])
                    halves.append(ko)
                    if blk == 1 and hf == 0 and "ps1" in dbg_d:
                        t9 = nxt("tmp", 4)
                        CP("dve", tmp[t9][:], PSt[ko][:], [b_PS[ko]], [b_tmp[t9]])
                        dump("ps1", tmp[t9][:], [b_tmp[t9]])
                resid_norm(l, blk, halves, None, gff)
            dump(f"hT2_{l}", hT[:], b_hT)
            dump(f"xmid_{l}", xs_d, b_xs)

            fence(RA_P3, RA_P5)
            fence(RB_P3, RB_P5)
            P.dma("pool", wdnt, wdn[l], writes=[b_wdn], stream="w")
            if l == nl - 1:
                P.dma("sp", gfin_t, gfin_d.partition_broadcast(128), writes=b_wtmh)
            cv = convp[:, l * 176:(l + 1) * 176].rearrange("p (c k) -> p c k", c=44)
            hal = lt[:].rearrange("p b h -> p (b h)")[:, 0:88].rearrange("p (c k) -> p c k", c=44)
            last = (l == nl - 1)

            def down_proj(th):
                for bl in range(8):
                    blk = th * 8 + bl
                    halves = []
                    for hf in range(2):
                        ko = 4 + nxt("acc", 4)
                        for kc in range(22):
                            MM(PSt[ko][:], actT[:, kc, bl * 128:(bl + 1) * 128], wdnt[:, kc, hf * 512:(hf + 1) * 512], kc == 0, kc == 21,
                               [b_act[kc][bl // 4], b_wdn], [b_PS[ko]])
                        halves.append(ko)
                    resid_norm(l + 1 if not last else 0, blk, halves, None, gat, final=last)

            pend2 = []

            def stage2():
                while pend2:
                    cc_, tcl_, ta, tb = pend2.pop(0)
                    ACT(tmp[ta][:], tmp[ta][:], AF.Silu, [b_tmp[ta]], [b_tmp[ta]])
                    TT("pool", actT[:, cc_, tcl_ * 512:(tcl_ + 1) * 512], tmp[ta][:], tmp[tb][:], ALU.mult,
                       [b_tmp[ta], b_tmp[tb]], [b_act[cc_][tcl_]])

            def unit_ffn(th, cc):
                def ld():
                    return load_fm_up(l, cc), load_fm_up(l, 22 + cc)

                def comp(hd):
                    (wa, bwa), (wb_, bwb) = hd
                    for tcl in range(2):
                        tc = th * 2 + tcl
                        res = []
                        for which, (wt, bw, chn) in enumerate(((wa, bwa, cc), (wb_, bwb, 22 + cc))):
                            kp = proj_fm_mm(wt, bw, tc)
                            ku = which * 2 + (tcl % 2)
                            CP("act", ubuf[:, ku, 2:514], PSt[kp][:], [b_PS[kp]], [b_ub[ku]])
                        for which, chn in enumerate((cc, 22 + cc)):
                            ku = which * 2 + (tcl % 2)
                            u, bu, buh = ubuf[:, ku, :], b_ub[ku], b_ubh[ku]
                            w0, w1, w2, bb = (cv[:, chn, q_:q_ + 1] for q_ in range(4))
                            t1 = nxt("tmp", 4)
                            ACT(tmp[t1][:], u[:, 2:514], AF.Identity, [bu, b_par], [b_tmp[t1]], scale=w2, bias=bb)
                            if tc == 0:
                                P.op("dve", lambda e, u=u: e.memset(u[:, 0:2], 0.0), reads=[], writes=[buh])
                            elif tcl == 1:
                                kprev = which * 2
                                CP("dve", u[:, 0:2], ubuf[:, kprev, 512:514], [b_ub[kprev]], [buh])
                            else:
                                CP("dve", u[:, 0:2], hal[:, chn, :], [b_lt], [buh])
                            if tc == 1:
                                CP("dve", hal[:, chn, :], u[:, 512:514], [bu], [b_lt])
                            STT(tmp[t1][:], u[:, 1:513], w1, tmp[t1][:], ALU.mult, ALU.add, [bu, buh, b_par, b_tmp[t1]], [b_tmp[t1]])
                            STT(tmp[t1][:], u[:, 0:512], w0, tmp[t1][:], ALU.mult, ALU.add, [bu, buh, b_par, b_tmp[t1]], [b_tmp[t1]])
                            res.append(t1)
                        stage2()
                        pend2.append((cc, tcl, res[0], res[1]))
                    if cc == 21:
                        stage2()
                        down_proj(th)
                return ld, comp

            unitsF = [unit_ffn(th, cc) for th in range(2) for cc in range(22)]
            run_units(unitsF, pre.pop("F", None), depth=1)
            fence(RA_P5, RA_P2)
            fence(RB_P5, RB_P2)
            NMMv[0] = 3

        rr["acc"] = 0
        gfin_t = wtmt[:].rearrange("p k c -> p (k c)")[:, 0:2048].bitcast(F32)
        ubuf_p3 = [xb[0][:, 0:512], xb[0][:, 512:1024], xb[1][:, 0:512], xb[1][:, 512:1024]]
        b_macc = [b_xb[0], b_xb[0], b_xb[1], b_xb[1]]

        def load_fm_up(l, ch):
            k = nxt("wst", 4)
            P.dma("pool", wst[k][:], wup[l, ch], writes=[b_wst[k]], stream="w")
            return wst[k], b_wst[k]

        for blk in range(NB):
            resid_norm(0, blk, None, None, gat, first=True)
        dump("hT0", hT[:], b_hT)
        for l in range(nl):
            layer(l)
        P.wait_all("sp", b_out + [b_dbg])
        P.finish()
    return nc


def prep_weights(attn_norm, w_in, forget_bias, ret_norm, attn_sinks, w_branch, w_out,
                 ffn_norm, w_up, conv_w, conv_b, w_down, final_norm):
    f = lambda a: np.ascontiguousarray(np.asarray(a, dtype=np.float32))
    w_in = f(w_in)
    w_in = np.concatenate([w_in, np.zeros((w_in.shape[0], w_in.shape[1], 1), np.float32)], axis=2)
    fm_cols = np.stack([c for _, c in FM])
    wfm = w_in[:, :, fm_cols]
    wfm = wfm.reshape(L, 8, 128, NFM, 128).transpose(0, 3, 2, 1, 4)
    tm_cols = np.concatenate([c for _, c in TM_GROUPS])
    wtm = w_in[:, :, tm_cols].reshape(L, 8, 128, NTM).transpose(0, 2, 1, 3)
    wbr = f(w_branch).reshape(L, 4, 4, 128, 8, 128).transpose(0, 1, 4, 3, 2, 5)
    wout = f(w_out).reshape(L, 8, 128, 1024).transpose(0, 2, 1, 3)
    wup = f(w_up).reshape(L, 8, 128, 44, 128).transpose(0, 3, 2, 1, 4)
    wdn = f(w_down).reshape(L, 22, 128, 1024).transpose(0, 2, 1, 3)
    gattn = f(attn_norm).reshape(L, 8, 128).transpose(2, 0, 1).reshape(128, L * 8)
    gffn = f(ffn_norm).reshape(L, 8, 128).transpose(2, 0, 1).reshape(128, L * 8)
    cw = f(conv_w).reshape(L, 3, 44, 128)
    cb = f(conv_b).reshape(L, 1, 44, 128)
    convp = np.concatenate([cw, cb], axis=1).transpose(3, 0, 2, 1).reshape(128, L * 44 * 4)
    return {
        "wfm": f(wfm), "wtm": f(wtm), "wbr": f(wbr), "wout": f(wout), "wup": f(wup), "wdn": f(wdn),
        "gattn": f(gattn), "gffn": f(gffn), "gfin": f(final_norm).reshape(1, 1024),
        "fbias": f(forget_bias).reshape(1, L * 8), "retn": f(ret_norm).reshape(L, 1, 512),
        "sinks": f(attn_sinks).reshape(1, L * 8), "convp": f(convp),
    }


_NC_CACHE = {}


def kernel(x, attn_norm, w_in, forget_bias, ret_norm, attn_sinks, w_branch, w_out,
           ffn_norm, w_up, conv_w, conv_b, w_down, final_norm):
    x = np.asarray(x, dtype=np.float32)
    shared = prep_weights(attn_norm, w_in, forget_bias, ret_norm, attn_sinks, w_branch, w_out,
                          ffn_norm, w_up, conv_w, conv_b, w_down, final_norm)
    if "nc" not in _NC_CACHE:
        _NC_CACHE["nc"] = build_program()
    nc = _NC_CACHE["nc"]
    n = x.shape[0]
    in_maps = [dict(shared, x=np.ascontiguousarray(x[b])) for b in range(n)]
    res = run_bass_kernel_spmd(nc, in_maps, core_ids=list(range(n)))
    return np.stack([np.asarray(r["out"], dtype=np.float32) for r in res.results], axis=0)
```
